# Optimizing a Trainium2 kernel written in Bass

```python
import math
import jax, jax.numpy as jnp
from jax import lax
import numpy as np

D_MODEL = 2048
BATCH = 4
SEQ = 4096
DEPTH = 2

D_FF = 5632
SSM_HEADS = 16
SSM_HEAD_DIM = 64
SSM_D_INNER = SSM_HEADS * SSM_HEAD_DIM
SSM_GROUPS = 2
SSM_STATE = 128
SSM_CONV = 4
SSM_CONV_DIM = SSM_D_INNER + 2 * SSM_GROUPS * SSM_STATE
SSD_CHUNK = 256
MLA_HEADS = 8
MLA_Q_LORA = 768
MLA_KV_LORA = 512
MLA_NOPE = 128
MLA_ROPE = 64
MLA_QK_DIM = MLA_NOPE + MLA_ROPE
MLA_V = 128
MLA_WIDTH = MLA_HEADS * MLA_V
ATTN_Q_BLOCK = 128
RET_HEADS = 4
RET_QK_HEAD = 256
RET_V_HEAD = 256
RET_QK = RET_HEADS * RET_QK_HEAD
RET_V = RET_HEADS * RET_V_HEAD
RET_CHUNK = 256
N_BRANCH = 3
ROPE_THETA = 10000.0
NORM_EPS = 1e-6
IN_SPLITS = (SSM_D_INNER, SSM_CONV_DIM, SSM_HEADS,
             MLA_Q_LORA, MLA_KV_LORA + MLA_ROPE,
             RET_QK, RET_QK, RET_V, RET_V,
             N_BRANCH * D_MODEL)
D_IN = sum(IN_SPLITS)

kernel_name = 'hybrid_ssd_mla_retention_macaron'


def rmsnorm(x, w):
    xf = x.astype(jnp.float32)
    y = xf * lax.rsqrt(jnp.mean(xf * xf, axis=-1, keepdims=True) + NORM_EPS)
    return (y * w.astype(jnp.float32)).astype(x.dtype)


def swiglu(x, w_gate, w_up, w_down):
    return (jax.nn.silu(x @ w_gate) * (x @ w_up)) @ w_down


def rope_tables(positions, dim):
    inv = 1.0 / (ROPE_THETA ** (jnp.arange(0, dim, 2, dtype=jnp.float32) / dim))
    ang = positions.astype(jnp.float32)[..., None] * inv
    return jnp.cos(ang), jnp.sin(ang)


def apply_rope(x, cos, sin):
    x1, x2 = jnp.split(x, 2, axis=-1)
    c = cos[:, :, None, :]
    s = sin[:, :, None, :]
    return jnp.concatenate([x1 * c - x2 * s, x1 * s + x2 * c], axis=-1).astype(x.dtype)


def causal_depthwise_conv(x, w, b):
    k = w.shape[0]
    out = lax.conv_general_dilated(
        x, w[:, None, :].astype(x.dtype), window_strides=(1,), padding=[(k - 1, 0)],
        dimension_numbers=('NWC', 'WIO', 'NWC'), feature_group_count=x.shape[-1])
    return out + b


def ssd_chunked(xs, dt, a, bm, cm):
    b, s, h, p = xs.shape
    g, n = bm.shape[2], bm.shape[3]
    r = h // g
    L = math.gcd(s, SSD_CHUNK)
    nc = s // L
    xd = (xs * dt[..., None]).reshape(b, nc, L, g, r, p)
    adt = (dt * a).reshape(b, nc, L, g, r)
    bm = bm.reshape(b, nc, L, g, n)
    cm = cm.reshape(b, nc, L, g, n)
    acs = jnp.cumsum(adt, axis=2)
    causal = jnp.tril(jnp.ones((L, L), dtype=bool))[None, None, :, :, None, None]
    seg = acs[:, :, :, None] - acs[:, :, None, :]
    decay = jnp.exp(jnp.where(causal, seg, -jnp.inf))
    cb = jnp.einsum('bctgn,bcsgn->bctsg', cm, bm)
    y_diag = jnp.einsum('bctsgr,bcsgrp->bctgrp', cb[..., None] * decay, xd)
    xdd = xd * jnp.exp(acs[:, :, -1:] - acs)[..., None]
    states = jnp.einsum('bcsgn,bcsgrp->bcgrpn', bm, xdd)
    chunk_decay = jnp.exp(acs[:, :, -1])

    def carry_state(state, inp):
        st, dec = inp
        return state * dec[..., None, None] + st, state

    init = jnp.zeros((b, g, r, p, n), states.dtype)
    _, prev = lax.scan(carry_state, init,
                       (jnp.swapaxes(states, 0, 1), jnp.swapaxes(chunk_decay, 0, 1)))
    prev = jnp.swapaxes(prev, 0, 1)
    y_off = jnp.einsum('bctgn,bcgrpn->bctgrp', cm, prev) * jnp.exp(acs)[..., None]
    return (y_diag + y_off).reshape(b, s, h, p)


def retention_chunked(q, k, v):
    b, s, h, dk = q.shape
    dv = v.shape[-1]
    L = math.gcd(s, RET_CHUNK)
    nc = s // L
    expo = 5.0 + 7.0 * jnp.arange(h, dtype=jnp.float32) / (h - 1)
    log_gamma = jnp.log1p(-jnp.exp2(-expo))
    pos = jnp.arange(L, dtype=jnp.float32)
    rel = pos[:, None] - pos[None, :]
    dmask = jnp.where(rel[None] >= 0, jnp.exp(rel[None] * log_gamma[:, None, None]), 0.0).astype(q.dtype)
    q = q.reshape(b, nc, L, h, dk)
    k = k.reshape(b, nc, L, h, dk)
    v = v.reshape(b, nc, L, h, dv)
    scores = jnp.einsum('bcthd,bcshd->bchts', q, k) * dmask
    y_in = jnp.einsum('bchts,bcshe->bcthe', scores, v)
    k_dec = jnp.exp((L - 1 - pos)[:, None] * log_gamma).astype(q.dtype)
    states = jnp.einsum('bcshd,bcshe->bchde', k * k_dec[None, None, :, :, None], v)
    chunk_decay = jnp.exp(L * log_gamma).astype(q.dtype)

    def carry_state(state, st):
        return state * chunk_decay[None, :, None, None] + st, state

    init = jnp.zeros((b, h, dk, dv), states.dtype)
    _, prev = lax.scan(carry_state, init, jnp.swapaxes(states, 0, 1))
    prev = jnp.swapaxes(prev, 0, 1)
    q_dec = jnp.exp((pos + 1)[:, None] * log_gamma).astype(q.dtype)
    y_cross = jnp.einsum('bcthd,bchde->bcthe', q, prev) * q_dec[None, None, :, :, None]
    return (y_in + y_cross).reshape(b, s, h, dv)


def causal_attention_blocks(q, k, v, scale):
    b, s, h, d = q.shape
    nb = s // ATTN_Q_BLOCK
    qb = jnp.swapaxes(q.reshape(b, nb, ATTN_Q_BLOCK, h, d), 0, 1)
    kpos = jnp.arange(s)

    def one_block(args):
        qi, i = args
        sc = jnp.einsum('bqhd,bkhd->bhqk', qi, k).astype(jnp.float32) * scale
        qpos = i * ATTN_Q_BLOCK + jnp.arange(ATTN_Q_BLOCK)
        sc = jnp.where(kpos[None, :] <= qpos[:, None], sc, -jnp.inf)
        pr = jax.nn.softmax(sc, axis=-1).astype(v.dtype)
        return jnp.einsum('bhqk,bkhe->bqhe', pr, v)

    out = lax.map(one_block, (qb, jnp.arange(nb)))
    return jnp.swapaxes(out, 0, 1).reshape(b, s, h, v.shape[-1])


def ssm_branch(z, xbc, dt_raw, conv_w, conv_b, dt_bias, a_log, d_skip, ssm_norm):
    b, s, _ = z.shape
    xbc = jax.nn.silu(causal_depthwise_conv(xbc, conv_w, conv_b))
    xs, bm, cm = jnp.split(xbc, [SSM_D_INNER, SSM_D_INNER + SSM_GROUPS * SSM_STATE], axis=-1)
    xs = xs.reshape(b, s, SSM_HEADS, SSM_HEAD_DIM)
    bm = bm.reshape(b, s, SSM_GROUPS, SSM_STATE)
    cm = cm.reshape(b, s, SSM_GROUPS, SSM_STATE)
    dt = jax.nn.softplus(dt_raw.astype(jnp.float32) + dt_bias.astype(jnp.float32))
    a = -jnp.exp(a_log.astype(jnp.float32))
    y = ssd_chunked(xs, dt, a, bm, cm) + d_skip[:, None] * xs
    y = y.reshape(b, s, SSM_D_INNER) * jax.nn.silu(z)
    y = rmsnorm(y.reshape(b, s, SSM_GROUPS, SSM_D_INNER // SSM_GROUPS),
                ssm_norm.reshape(SSM_GROUPS, SSM_D_INNER // SSM_GROUPS))
    return y.reshape(b, s, SSM_D_INNER).astype(z.dtype)


def mla_branch(q_lat, kv_lat, cos, sin, q_a_norm, w_q_b, kv_a_norm, w_kv_b, q_norm, k_norm):
    b, s, _ = q_lat.shape
    q = (rmsnorm(q_lat, q_a_norm) @ w_q_b).reshape(b, s, MLA_HEADS, MLA_QK_DIM)
    c_kv, k_pe = jnp.split(kv_lat, [MLA_KV_LORA], axis=-1)
    kv = (rmsnorm(c_kv, kv_a_norm) @ w_kv_b).reshape(b, s, MLA_HEADS, MLA_NOPE + MLA_V)
    k_nope, v = jnp.split(kv, [MLA_NOPE], axis=-1)
    k = jnp.concatenate([k_nope, jnp.broadcast_to(k_pe[:, :, None, :], (b, s, MLA_HEADS, MLA_ROPE))], axis=-1)
    q = rmsnorm(q, q_norm)
    k = rmsnorm(k, k_norm)
    q = jnp.concatenate([q[..., :MLA_NOPE], apply_rope(q[..., MLA_NOPE:], cos, sin)], axis=-1)
    k = jnp.concatenate([k[..., :MLA_NOPE], apply_rope(k[..., MLA_NOPE:], cos, sin)], axis=-1)
    o = causal_attention_blocks(q, k, v, MLA_QK_DIM ** -0.5)
    return o.reshape(b, s, MLA_WIDTH)


def retention_branch(rq, rk, rv, rg, cos, sin, ret_norm):
    b, s, _ = rq.shape
    q = apply_rope(rq.reshape(b, s, RET_HEADS, RET_QK_HEAD), cos, sin)
    k = apply_rope(rk.reshape(b, s, RET_HEADS, RET_QK_HEAD), cos, sin) * (RET_QK_HEAD ** -0.5)
    v = rv.reshape(b, s, RET_HEADS, RET_V_HEAD)
    y = retention_chunked(q, k, v)
    y = rmsnorm(y, ret_norm.reshape(RET_HEADS, RET_V_HEAD)).reshape(b, s, RET_V)
    return jax.nn.silu(rg) * y


def hybrid_mixer(h, cos_mla, sin_mla, cos_ret, sin_ret, w_in, gate_b, conv_w, conv_b,
                 dt_bias, a_log, d_skip, ssm_norm, q_a_norm, w_q_b, kv_a_norm, w_kv_b,
                 q_norm, k_norm, ret_norm, w_br_ssm, w_br_mla, w_br_ret, w_out):
    b, s, d = h.shape
    proj = h @ w_in
    idx = np.cumsum(IN_SPLITS)[:-1].tolist()
    z, xbc, dt_raw, q_lat, kv_lat, rq, rk, rv, rg, gates = jnp.split(proj, idx, axis=-1)
    y_ssm = ssm_branch(z, xbc, dt_raw, conv_w, conv_b, dt_bias, a_log, d_skip, ssm_norm)
    y_mla = mla_branch(q_lat, kv_lat, cos_mla, sin_mla, q_a_norm, w_q_b, kv_a_norm, w_kv_b, q_norm, k_norm)
    y_ret = retention_branch(rq, rk, rv, rg, cos_ret, sin_ret, ret_norm)
    g = jax.nn.sigmoid((gates + gate_b).astype(jnp.float32)).astype(h.dtype).reshape(b, s, N_BRANCH, d)
    merged = (g[:, :, 0] * (y_ssm @ w_br_ssm)
              + g[:, :, 1] * (y_mla @ w_br_mla)
              + g[:, :, 2] * (y_ret @ w_br_ret))
    return merged @ w_out


def setup_inputs(seed: int = 0) -> dict:
    key = jax.random.key(seed)
    ks = iter(jax.random.split(key, 48))

    def normal(shape, scale):
        return scale * jax.random.normal(next(ks), shape, jnp.float32)

    def gain(shape):
        return 1.0 + normal(shape, 0.05)

    L_ = DEPTH
    x = normal((BATCH, SEQ, D_MODEL), 1.0)
    offset = jax.random.randint(next(ks), (BATCH, 1), 0, 1024, dtype=jnp.int32)
    positions = (offset + jnp.arange(SEQ, dtype=jnp.int32)[None, :]).astype(jnp.int32)
    dt0 = jnp.exp(jax.random.uniform(next(ks), (L_, SSM_HEADS), jnp.float32,
                                     math.log(1e-3), math.log(1e-1)))
    dt_bias = dt0 + jnp.log(-jnp.expm1(-dt0))
    a_log = jnp.log(jax.random.uniform(next(ks), (L_, SSM_HEADS), jnp.float32, 1.0, 16.0))
    return {
        'x': x,
        'positions': positions,
        'ffn1_norm': gain((L_, D_MODEL)),
        'ffn1_w_gate': normal((L_, D_MODEL, D_FF), D_MODEL ** -0.5),
        'ffn1_w_up': normal((L_, D_MODEL, D_FF), D_MODEL ** -0.5),
        'ffn1_w_down': normal((L_, D_FF, D_MODEL), D_FF ** -0.5),
        'mix_norm': gain((L_, D_MODEL)),
        'w_in': normal((L_, D_MODEL, D_IN), D_MODEL ** -0.5),
        'gate_b': normal((L_, N_BRANCH * D_MODEL), 0.01),
        'conv_w': normal((L_, SSM_CONV, SSM_CONV_DIM), SSM_CONV ** -0.5),
        'conv_b': normal((L_, SSM_CONV_DIM), 0.01),
        'dt_bias': dt_bias,
        'a_log': a_log,
        'd_skip': gain((L_, SSM_HEADS)),
        'ssm_norm': gain((L_, SSM_D_INNER)),
        'q_a_norm': gain((L_, MLA_Q_LORA)),
        'w_q_b': normal((L_, MLA_Q_LORA, MLA_HEADS * MLA_QK_DIM), MLA_Q_LORA ** -0.5),
        'kv_a_norm': gain((L_, MLA_KV_LORA)),
        'w_kv_b': normal((L_, MLA_KV_LORA, MLA_HEADS * (MLA_NOPE + MLA_V)), MLA_KV_LORA ** -0.5),
        'q_norm': gain((L_, MLA_QK_DIM)),
        'k_norm': gain((L_, MLA_QK_DIM)),
        'ret_norm': gain((L_, RET_V)),
        'w_br_ssm': normal((L_, SSM_D_INNER, D_MODEL), SSM_D_INNER ** -0.5),
        'w_br_mla': normal((L_, MLA_WIDTH, D_MODEL), MLA_WIDTH ** -0.5),
        'w_br_ret': normal((L_, RET_V, D_MODEL), RET_V ** -0.5),
        'w_out': normal((L_, D_MODEL, D_MODEL), D_MODEL ** -0.5),
        'ffn2_norm': gain((L_, D_MODEL)),
        'ffn2_w_gate': normal((L_, D_MODEL, D_FF), D_MODEL ** -0.5),
        'ffn2_w_up': normal((L_, D_MODEL, D_FF), D_MODEL ** -0.5),
        'ffn2_w_down': normal((L_, D_FF, D_MODEL), D_FF ** -0.5),
    }


def reference(x, positions, ffn1_norm, ffn1_w_gate, ffn1_w_up, ffn1_w_down, mix_norm, w_in,
              gate_b, conv_w, conv_b, dt_bias, a_log, d_skip, ssm_norm, q_a_norm, w_q_b,
              kv_a_norm, w_kv_b, q_norm, k_norm, ret_norm, w_br_ssm, w_br_mla, w_br_ret,
              w_out, ffn2_norm, ffn2_w_gate, ffn2_w_up, ffn2_w_down):
    cos_mla, sin_mla = rope_tables(positions, MLA_ROPE)
    cos_ret, sin_ret = rope_tables(positions, RET_QK_HEAD)
    for l in range(DEPTH):
        x = x + 0.5 * swiglu(rmsnorm(x, ffn1_norm[l]), ffn1_w_gate[l], ffn1_w_up[l], ffn1_w_down[l])
        x = x + hybrid_mixer(rmsnorm(x, mix_norm[l]), cos_mla, sin_mla, cos_ret, sin_ret,
                             w_in[l], gate_b[l], conv_w[l], conv_b[l], dt_bias[l], a_log[l],
                             d_skip[l], ssm_norm[l], q_a_norm[l], w_q_b[l], kv_a_norm[l],
                             w_kv_b[l], q_norm[l], k_norm[l], ret_norm[l], w_br_ssm[l],
                             w_br_mla[l], w_br_ret[l], w_out[l])
        x = x + 0.5 * swiglu(rmsnorm(x, ffn2_norm[l]), ffn2_w_gate[l], ffn2_w_up[l], ffn2_w_down[l])
    return x
```

```python
import numpy as np
import concourse.bass as bass
import concourse.mybir as mybir
from concourse.bass_utils import run_bass_kernel_spmd
from contextlib import ExitStack

F32 = mybir.dt.float32
BF16 = mybir.dt.bfloat16
I32 = mybir.dt.int32
AF = mybir.ActivationFunctionType
ALU = mybir.AluOpType
AX = mybir.AxisListType

ENGINES = ("pe", "act", "dve", "pool", "sp")
NDMA_SEMS = {"sp": 12, "pool": 8, "act": 4}


class Res:
    __slots__ = ("name", "w", "rs")

    def __init__(self, name):
        self.name = name
        self.w = None
        self.rs = []


class Op:
    __slots__ = ("eng", "emit", "deps", "signal", "is_dma", "idx", "sem", "tick", "prewait")

    def __init__(self, eng, emit, is_dma):
        self.eng = eng
        self.emit = emit
        self.is_dma = is_dma
        self.deps = []
        self.signal = is_dma
        self.sem = None
        self.tick = None
        self.prewait = None


class Prog:
    def __init__(self, nc):
        self.nc = nc
        self.gstack = ExitStack()
        self.sem_eng = {e: self.gstack.enter_context(nc.semaphore("s_" + e)) for e in ENGINES}
        self.sem_dma = {q: [self.gstack.enter_context(nc.semaphore("d_%s%d" % (q, i))) for i in range(n)]
                        for q, n in NDMA_SEMS.items()}
        self.cnt = {e: 0 for e in ENGINES}
        self.dcnt = {e: 0 for e in ENGINES}
        self.cum = {e: {} for e in ENGINES}
        self.waited = {e: {} for e in ENGINES}
        self.stage_id = 0
        self._reset()

    def _reset(self):
        self.streams = {e: [] for e in ENGINES}
        self.res = {}
        self.stack = ExitStack()

    def R(self, *key):
        r = self.res.get(key)
        if r is None:
            r = Res(key)
            self.res[key] = r
        return r

    def sb(self, name, shape, dt):
        return self.stack.enter_context(self.nc.sbuf_tensor("g%d_%s" % (self.stage_id, name), list(shape), dt))

    def ps(self, name, shape, dt=F32):
        return self.stack.enter_context(self.nc.psum_tensor("g%d_%s" % (self.stage_id, name), list(shape), dt))

    def op(self, eng, emit, reads=(), writes=(), dma=False):
        o = Op(eng, emit, dma)
        deps = set()
        for r in reads:
            if r.w is not None:
                deps.add(r.w)
        for r in writes:
            if r.w is not None:
                deps.add(r.w)
            for q in r.rs:
                deps.add(q)
        latest = {}
        for d in deps:
            if d is o:
                continue
            if d.is_dma:
                o.deps.append(d)
                continue
            if (not dma) and d.eng == eng and eng == "pe":
                continue
            q = latest.get(d.eng)
            if q is None or d.idx > q.idx:
                latest[d.eng] = d
        for d in latest.values():
            d.signal = True
            o.deps.append(d)
        o.idx = len(self.streams[eng])
        for r in reads:
            r.rs.append(o)
        for r in writes:
            r.w = o
            r.rs = []
        self.streams[eng].append(o)
        return o

    def dma(self, q, out, in_, reads=(), writes=()):
        return self.op(q, lambda e: e.dma_start(out=out, in_=in_), reads, writes, dma=True)

    def end_stage(self, final=False):
        nc = self.nc
        for e in ENGINES:
            last_compute = None
            for o in self.streams[e]:
                if not o.is_dma:
                    last_compute = o
            if last_compute is not None:
                last_compute.signal = True
        for e in ENGINES:
            for o in self.streams[e]:
                if o.is_dma:
                    i = self.dcnt[e] % NDMA_SEMS[e]
                    self.dcnt[e] += 1
                    s = self.sem_dma[e][i]
                    prev = self.cum[e].get(i, 0)
                    o.prewait = (s, prev) if prev > 0 else None
                    self.cum[e][i] = prev + 16
                    o.sem, o.tick = s, self.cum[e][i]
                elif o.signal:
                    self.cnt[e] += 1
                    o.sem, o.tick = self.sem_eng[e], self.cnt[e]
        targets = []
        for e in ENGINES:
            if self.cnt[e] > 0:
                targets.append((self.sem_eng[e], self.cnt[e]))
            for i, v in self.cum[e].items():
                targets.append((self.sem_dma[e][i], v))

        def run_stream(ename, eng):
            waited = self.waited[ename]
            for o in self.streams[ename]:
                ws = []
                if o.prewait is not None:
                    ws.append(o.prewait)
                for d in o.deps:
                    ws.append((d.sem, d.tick))
                need = []
                for s, v in ws:
                    k = id(s)
                    if waited.get(k, 0) >= v:
                        continue
                    waited[k] = v
                    need.append((s, v))
                emb = None
                if need and not o.is_dma:
                    emb = need.pop()
                for s, v in need:
                    eng.wait_ge(s, v)
                ins = o.emit(eng)
                if emb is not None:
                    ins._wait_ge(emb[0], emb[1])
                if o.sem is not None:
                    ins.then_inc(o.sem, 16 if o.is_dma else 1)
            for s, v in targets:
                if waited.get(id(s), 0) < v:
                    waited[id(s)] = v
                    eng.wait_ge(s, v)

        with nc.Block() as block:
            @block.sync
            def _(e):
                run_stream("sp", e)

            @block.tensor
            def _(e):
                run_stream("pe", e)

            @block.scalar
            def _(e):
                run_stream("act", e)

            @block.vector
            def _(e):
                run_stream("dve", e)

            @block.gpsimd
            def _(e):
                run_stream("pool", e)
        self.stack.close()
        self.stage_id += 1
        self._reset()
        if final:
            self.gstack.close()

    def emit_all(self):
        self.end_stage(final=True)


import ml_dtypes
import math

NPBF = ml_dtypes.bfloat16
NEG = -30000.0


class PsPool:
    def __init__(self, P, n=8):
        self.P = P
        self.t = [P.ps("psb%d" % i, [128, 512]) for i in range(n)]
        self.i = 0

    def get(self):
        i = self.i % len(self.t)
        self.i += 1
        return self.t[i], self.P.R("psb", i)


def host_consts():
    c = {}
    c["ident"] = np.eye(128, dtype=np.float32).astype(NPBF)
    s = np.arange(128)[:, None]
    t = np.arange(128)[None, :]
    c["negtri"] = np.where(t < s, NEG, 0.0).astype(NPBF)
    am = np.zeros((128, 4, 512), np.float32)
    tt = np.arange(512)[None, :]
    for j in range(4):
        am[:, j, :] = np.where(tt < s + 128 * j, NEG, 0.0)
    c["amask"] = am.astype(NPBF)
    sel = np.zeros((8, 8, 128), np.float32)
    for h in range(8):
        sel[h, h, :] = 1.0
    c["sel"] = sel
    c["tri"] = (s <= t).astype(np.float32)
    c["ones32"] = np.ones((128, 128), np.float32)
    invf = np.zeros((128, 2), np.float32)
    invf[:, 0] = 1.0 / (10000.0 ** (np.arange(0, 256, 2, dtype=np.float32) / 256))
    im = 1.0 / (10000.0 ** (np.arange(0, 64, 2, dtype=np.float32) / 64))
    invf[:64, 1] = np.concatenate([im, im])
    c["invf"] = invf
    rm = np.zeros((64, 64), np.float32)
    for i in range(32):
        rm[i + 32, i] = -1.0
        rm[i, i + 32] = 1.0
    c["rmat"] = rm.astype(NPBF)
    expo = 5.0 + 7.0 * np.arange(4, dtype=np.float64) / 3.0
    lg = np.log1p(-np.exp2(-expo))
    dm = np.zeros((128, 4, 128), np.float32)
    qd = np.zeros((128, 4, 128), np.float32)
    kd = np.zeros((128, 4), np.float32)
    for h in range(4):
        dm[:, h, :] = np.where(t >= s, np.exp((t - s) * lg[h]), 0.0) / 16.0
        qd[:, h, :] = np.exp((t + 1) * lg[h])
        kd[:, h] = np.exp((127 - np.arange(128)) * lg[h]) / 16.0
    c["ret_dm"] = dm
    c["ret_qd"] = qd
    c["ret_kd"] = kd
    c["ret_cd"] = [float(np.exp(128 * lg[h])) for h in range(4)]
    return c


CONST_SHAPES = {"ident": ([128, 128], BF16), "negtri": ([128, 128], BF16), "amask": ([128, 4, 512], BF16),
                "sel": ([8, 8, 128], F32), "tri": ([128, 128], F32), "ones32": ([128, 128], F32),
                "invf": ([128, 2], F32), "rmat": ([64, 64], BF16)}


def declare_consts(nc):
    return {n: nc.dram_tensor("c_" + n, shp, dt, kind="ExternalInput").ap() for n, (shp, dt) in CONST_SHAPES.items()}


class Consts:
    def __init__(self, P, cdram, need=()):
        self.P = P
        self.ones_bf = P.sb("ones_bf", [128, 128], BF16)
        P.op("dve", lambda e: e.memset(self.ones_bf[:], 1.0), writes=[P.R("ones_bf")])
        self.eps_col = P.sb("eps_col", [128, 1], F32)
        P.op("dve", lambda e: e.memset(self.eps_col[:], 1e-6), writes=[P.R("eps_col")])
        for n in need:
            shp, dt = CONST_SHAPES[n]
            t = P.sb("cs_" + n, shp, dt)
            P.dma("sp", t[:], cdram[n], writes=[P.R("c_" + n)])
            setattr(self, n, t)


def rms_rstd(P, C, tag, ps_tile, psr, D, rstd_out, rstd_res, rows=128):
    P.op("act", lambda e: e.activation(out=rstd_out, in_=ps_tile, func=AF.Sqrt, scale=1.0 / D,
                                       bias=C.eps_col[0:rows, 0:1]),
         reads=[psr, P.R("eps_col")], writes=[rstd_res])
    P.op("dve", lambda e: e.reciprocal(out=rstd_out, in_=rstd_out), reads=[rstd_res], writes=[rstd_res])


TB = 512


def rms_stats(P, C, tag, src_tiles, nt, D, eps, ps_tile, sq_bufs, rstd_out, width=TB):
    for k in range(nt):
        ap, res = src_tiles(k)
        sq = sq_bufs[k % len(sq_bufs)]
        sqr = P.R(tag + "sq", k % len(sq_bufs))
        P.op("act", lambda e, sq=sq, ap=ap: e.activation(out=sq, in_=ap, func=AF.Square),
             reads=[res], writes=[sqr])
        P.op("pe", lambda e, sq=sq, k=k: e.matmul(ps_tile, lhsT=C.ones_bf[:], rhs=sq,
                                                  start=(k == 0), stop=(k == nt - 1)),
             reads=[sqr, P.R("ones_bf")], writes=[P.R(tag + "ps")])
    P.op("act", lambda e: e.activation(out=rstd_out, in_=ps_tile, func=AF.Sqrt, scale=1.0 / D, bias=C.eps_col[:, 0:1]),
         reads=[P.R(tag + "ps"), P.R("eps_col")], writes=[P.R(tag + "rstd")])
    P.op("dve", lambda e: e.reciprocal(out=rstd_out, in_=rstd_out),
         reads=[P.R(tag + "rstd")], writes=[P.R(tag + "rstd")])


def ffn_stage(P, C, PS, tag, xT_in, xT_out, normw, Wg, Wu, Wd, T, D=2048, FF=5632, HB=256, eps=1e-6):
    nc = P.nc
    KD = D // 128
    NHB = FF // HB
    HT = HB // 128
    x32 = P.sb(tag + "x32", [128, KD, TB], F32)
    xn = P.sb(tag + "xn", [128, KD, TB], BF16)
    acc = P.sb(tag + "acc", [128, KD, TB], F32)
    wcol = P.sb(tag + "wcol", [128, KD], F32)
    rstd = P.sb(tag + "rstd", [128, TB], F32)
    sq = [P.sb(tag + "sq%d" % i, [128, TB], BF16) for i in range(2)]
    wg = [P.sb(tag + "wg%d" % i, [128, KD, HB], BF16) for i in range(2)]
    wu = [P.sb(tag + "wu%d" % i, [128, KD, HB], BF16) for i in range(2)]
    wd = [P.sb(tag + "wd%d" % i, [128, HT, D], BF16) for i in range(2)]
    hb = [P.sb(tag + "h%d" % i, [128, HT, TB], BF16) for i in range(2)]
    sg = [P.sb(tag + "sg%d" % i, [128, TB], F32) for i in range(2)]
    ps_g = [P.ps(tag + "psg%d" % i, [128, TB]) for i in range(2)]
    ps_u = [P.ps(tag + "psu%d" % i, [128, TB]) for i in range(2)]
    ps_d = [P.ps(tag + "psd%d" % i, [128, TB]) for i in range(2)]
    ps_s = P.ps(tag + "pss", [128, TB])

    P.dma("sp", wcol[:], normw, writes=[P.R(tag, "wcol")])
    Wg_v = Wg.rearrange("(k p) f -> p k f", p=128)
    Wu_v = Wu.rearrange("(k p) f -> p k f", p=128)
    Wd_v = Wd.rearrange("(k p) d -> p k d", p=128)
    xin_v = xT_in.rearrange("(k p) t -> p k t", p=128)
    xout_v = xT_out.rearrange("(k p) t -> p k t", p=128)
    gi = 0
    di = 0
    for b in range(T // TB):
        ts = slice(b * TB, (b + 1) * TB)
        P.dma("sp", x32[:], xin_v[:, :, ts], writes=[P.R(tag, "x32")])
        rms_stats(P, C, tag, lambda k: (x32[:, k, :], P.R(tag, "x32")), KD, D, eps,
                  ps_s[:], [s[:] for s in sq], rstd[:])
        for k in range(KD):
            P.op("dve", lambda e, k=k: e.scalar_tensor_tensor(
                out=xn[:, k, :], in0=x32[:, k, :], scalar=wcol[:, k:k + 1], in1=rstd[:],
                op0=ALU.mult, op1=ALU.mult),
                reads=[P.R(tag, "x32"), P.R(tag, "wcol"), P.R(tag + "rstd")], writes=[P.R(tag, "xn", k)])
        for j in range(NHB):
            wb = j % 2
            hs = slice(j * HB, (j + 1) * HB)
            P.dma("pool", wg[wb][:], Wg_v[:, :, hs], writes=[P.R(tag, "wg", wb)])
            P.dma("pool", wu[wb][:], Wu_v[:, :, hs], writes=[P.R(tag, "wu", wb)])
            P.dma("pool", wd[wb][:], Wd_v[:, j * HT:(j + 1) * HT, :], writes=[P.R(tag, "wd", wb)])
            for ht in range(HT):
                pb = gi % 2
                gi += 1
                for k in range(KD):
                    P.op("pe", lambda e, k=k, pb=pb, wb=wb, ht=ht: e.matmul(
                        ps_g[pb][:], lhsT=wg[wb][:, k, ht * 128:(ht + 1) * 128], rhs=xn[:, k, :],
                        start=(k == 0), stop=(k == KD - 1)),
                        reads=[P.R(tag, "wg", wb), P.R(tag, "xn", k)], writes=[P.R(tag, "psg", pb)])
                for k in range(KD):
                    P.op("pe", lambda e, k=k, pb=pb, wb=wb, ht=ht: e.matmul(
                        ps_u[pb][:], lhsT=wu[wb][:, k, ht * 128:(ht + 1) * 128], rhs=xn[:, k, :],
                        start=(k == 0), stop=(k == KD - 1)),
                        reads=[P.R(tag, "wu", wb), P.R(tag, "xn", k)], writes=[P.R(tag, "psu", pb)])
                P.op("act", lambda e, pb=pb: e.activation(out=sg[pb][:], in_=ps_g[pb][:], func=AF.Silu),
                     reads=[P.R(tag, "psg", pb)], writes=[P.R(tag, "sg", pb)])
                P.op("dve", lambda e, pb=pb, wb=wb, ht=ht: e.tensor_tensor(
                    out=hb[wb][:, ht, :], in0=sg[pb][:], in1=ps_u[pb][:], op=ALU.mult),
                    reads=[P.R(tag, "sg", pb), P.R(tag, "psu", pb)], writes=[P.R(tag, "h", wb, ht)])
            for dm in range(KD):
                pd = di % 2
                di += 1
                for ht in range(HT):
                    P.op("pe", lambda e, pd=pd, wb=wb, ht=ht, dm=dm: e.matmul(
                        ps_d[pd][:], lhsT=wd[wb][:, ht, dm * 128:(dm + 1) * 128], rhs=hb[wb][:, ht, :],
                        start=(ht == 0), stop=(ht == HT - 1)),
                        reads=[P.R(tag, "wd", wb), P.R(tag, "h", wb, ht)], writes=[P.R(tag, "psd", pd)])
                if j == 0:
                    P.op("dve", lambda e, pd=pd, dm=dm: e.tensor_copy(out=acc[:, dm, :], in_=ps_d[pd][:]),
                         reads=[P.R(tag, "psd", pd)], writes=[P.R(tag, "acc", dm)])
                else:
                    P.op("dve", lambda e, pd=pd, dm=dm: e.tensor_tensor(
                        out=acc[:, dm, :], in0=acc[:, dm, :], in1=ps_d[pd][:], op=ALU.add),
                        reads=[P.R(tag, "psd", pd), P.R(tag, "acc", dm)], writes=[P.R(tag, "acc", dm)])
        for dm in range(KD):
            P.op("dve", lambda e, dm=dm: e.scalar_tensor_tensor(
                out=acc[:, dm, :], in0=acc[:, dm, :], scalar=0.5, in1=x32[:, dm, :],
                op0=ALU.mult, op1=ALU.add),
                reads=[P.R(tag, "acc", dm), P.R(tag, "x32")], writes=[P.R(tag, "acc", dm)])
        P.dma("sp", xout_v[:, :, ts], acc[:], reads=[P.R(tag, "acc", dm) for dm in range(KD)],
              writes=[P.R(tag, "xout")])


TB = 512
PI = math.pi


def proj_stage(P, C, PS, io, T, D=2048):
    nc = P.nc
    KD = D // 128
    x32 = P.sb("p_x32", [128, KD, TB], F32)
    hn = P.sb("p_hn", [128, KD, TB], BF16)
    wcol = P.sb("p_wcol", [128, KD], F32)
    rstd = P.sb("p_rstd", [128, TB], F32)
    sq = [P.sb("p_sq%d" % i, [128, TB], BF16) for i in range(2)]
    NWB = 2
    wbuf = [P.sb("p_wb%d" % i, [128, KD, 512], BF16) for i in range(NWB)]
    ost = [P.sb("p_ost%d" % i, [128, 4, TB], BF16) for i in range(3)]
    ost32 = [P.sb("p_ost32_%d" % i, [128, 16], F32) for i in range(4)]
    dtst = P.sb("p_dtst", [16, TB], F32)
    wqn = P.sb("p_wqn", [128, 6, 1024], BF16)
    wqr = P.sb("p_wqr", [128, 6, 512], BF16)
    wkk = P.sb("p_wkk", [128, 4, 1024], BF16)
    wkv = P.sb("p_wkv", [128, 4, 1024], BF16)
    qan = P.sb("p_qan", [128, 6], F32)
    kvan = P.sb("p_kvan", [128, 4], F32)
    nrm = P.sb("p_nrm", [128, 4], F32)
    gb = P.sb("p_gb", [128, 48], F32)
    posi = P.sb("p_posi", [128, TB], I32)
    posf = P.sb("p_posf", [128, TB], F32)
    ang = P.sb("p_ang", [128, TB], F32)
    qf = P.sb("p_qf", [128, TB], F32)
    cosr = P.sb("p_cosr", [128, TB], F32)
    sinr = P.sb("p_sinr", [128, TB], F32)
    cosm = P.sb("p_cosm", [64, TB], F32)
    sinm = P.sb("p_sinm", [64, TB], F32)
    negpi = P.sb("p_negpi", [128, 1], F32)
    lat32 = x32[:, 0:6, :]
    latn = P.sb("p_latn", [128, 6, TB], BF16)
    ckv32 = x32[:, 6:10, :]
    ckvn = P.sb("p_ckvn", [128, 4, TB], BF16)
    kpe32 = P.sb("p_kpe32", [64, TB], F32)
    kbase = P.sb("p_kbase", [64, TB], F32)
    xr = P.sb("p_xr", [64, TB], BF16)
    t1 = [P.sb("p_t1_%d" % i, [128, TB], F32) for i in range(4)]
    hrstd = P.sb("p_hrstd", [128, TB], F32)

    P.dma("sp", wcol[:], io["mixw"], writes=[P.R("p_wcol")])
    P.dma("sp", qan[:], io["qan"], writes=[P.R("p_qan")])
    P.dma("sp", kvan[:], io["kvan"], writes=[P.R("p_kvan")])
    P.dma("sp", nrm[:], io["nrm"], writes=[P.R("p_nrm")])
    P.dma("sp", gb[:], io["gate_b"], writes=[P.R("p_gb")])
    P.op("dve", lambda e: e.memset(negpi[:], -PI), writes=[P.R("p_negpi")])
    P.dma("pool", wqn[:], io["wqb_n"].rearrange("(k p) c -> p k c", p=128), writes=[P.R("p_wqn")])
    P.dma("pool", wqr[:], io["wqb_r"].rearrange("(k p) c -> p k c", p=128), writes=[P.R("p_wqr")])
    P.dma("pool", wkk[:], io["wkvb_k"].rearrange("(k p) c -> p k c", p=128), writes=[P.R("p_wkk")])
    P.dma("pool", wkv[:], io["wkvb_v"].rearrange("(k p) c -> p k c", p=128), writes=[P.R("p_wkv")])
    Win = io["w_in"].rearrange("(k p) c -> p k c", p=128)
    xin_v = io["xT"].rearrange("(k p) t -> p k t", p=128)
    st = {"wi": 0, "oi": 0, "o32": 0, "ti": 0, "vi": 0}

    def tmp():
        i = st["ti"] % 4
        st["ti"] += 1
        return t1[i], P.R("p_t1", i)

    def load_w(c0, n):
        i = st["wi"] % NWB
        st["wi"] += 1
        P.dma("pool", wbuf[i][:, :, 0:n], Win[:, :, c0:c0 + n], writes=[P.R("p_wb", i)])
        return wbuf[i], P.R("p_wb", i)

    def fm_tiles(c0, n, cb):
        wb, wr = load_w(c0, n)
        for j in range((n + 127) // 128):
            rows = min(128, n - j * 128)
            ps, pr = PS.get()
            for k in range(KD):
                P.op("pe", lambda e, k=k, j=j, rows=rows, ps=ps, wb=wb: e.matmul(
                    ps[0:rows, :], lhsT=wb[:, k, j * 128:j * 128 + rows], rhs=hn[:, k, :],
                    start=(k == 0), stop=(k == KD - 1)),
                    reads=[wr, P.R("p_hn")], writes=[pr])
            cb(j, ps, pr, rows)

    def tm_tiles(c0, n, cb):
        wb, wr = load_w(c0, n)
        for sub in range(4):
            ps, pr = PS.get()
            for k in range(KD):
                P.op("pe", lambda e, k=k, sub=sub, ps=ps, wb=wb: e.matmul(
                    ps[:, 0:n], lhsT=hn[:, k, sub * 128:(sub + 1) * 128], rhs=wb[:, k, 0:n],
                    start=(k == 0), stop=(k == KD - 1)),
                    reads=[wr, P.R("p_hn")], writes=[pr])
            cb(sub, ps, pr)

    def ostage():
        i = st["oi"] % 3
        st["oi"] += 1
        return ost[i], P.R("p_ost", i)

    for b in range(T // TB):
        ts = slice(b * TB, (b + 1) * TB)
        P.dma("sp", x32[:], xin_v[:, :, ts], writes=[P.R("p_x32")])
        ps, pr = PS.get()
        for k in range(KD):
            s_ = sq[k % 2]
            P.op("act", lambda e, k=k, s_=s_: e.activation(out=s_[:], in_=x32[:, k, :], func=AF.Square),
                 reads=[P.R("p_x32")], writes=[P.R("p_sq", k % 2)])
            P.op("pe", lambda e, k=k, s_=s_, ps=ps: e.matmul(ps[:], lhsT=C.ones_bf[:], rhs=s_[:],
                                                           start=(k == 0), stop=(k == KD - 1)),
                 reads=[P.R("p_sq", k % 2), P.R("ones_bf")], writes=[pr])
        rms_rstd(P, C, "p", ps[:], pr, D, rstd[:], P.R("p_rstd"))
        for k in range(KD):
            P.op("dve", lambda e, k=k: e.scalar_tensor_tensor(
                out=hn[:, k, :], in0=x32[:, k, :], scalar=wcol[:, k:k + 1], in1=rstd[:],
                op0=ALU.mult, op1=ALU.mult),
                reads=[P.R("p_x32"), P.R("p_wcol"), P.R("p_rstd")], writes=[P.R("p_hn")])
        P.dma("sp", posi[:], io["pos"][0, ts].partition_broadcast(128), writes=[P.R("p_posi")])
        P.op("dve", lambda e: e.tensor_copy(out=posf[:], in_=posi[:]), reads=[P.R("p_posi")], writes=[P.R("p_posf")])
        for (col, rows, ct, sn, cres, sres) in ((0, 128, cosr, sinr, "p_cosr", "p_sinr"),
                                                (1, 64, cosm, sinm, "p_cosm", "p_sinm")):
            for (tab, res, shift) in ((sn, sres, 0.0), (ct, cres, 0.5 * PI)):
                P.op("dve", lambda e, col=col, rows=rows, shift=shift: e.tensor_scalar(
                    out=ang[0:rows, :], in0=posf[0:rows, :], scalar1=C.invf[0:rows, col:col + 1], scalar2=shift,
                    op0=ALU.mult, op1=ALU.add), reads=[P.R("p_posf"), P.R("c_invf")], writes=[P.R("p_ang")])
                P.op("dve", lambda e, rows=rows: e.tensor_scalar(
                    out=posi[0:rows, :], in0=ang[0:rows, :], scalar1=1.0 / (2 * PI), scalar2=None, op0=ALU.mult),
                    reads=[P.R("p_ang"), P.R("p_posf")], writes=[P.R("p_posi")])
                P.op("dve", lambda e, rows=rows: e.tensor_copy(out=qf[0:rows, :], in_=posi[0:rows, :]),
                     reads=[P.R("p_posi")], writes=[P.R("p_qf")])
                P.op("dve", lambda e, rows=rows: e.scalar_tensor_tensor(
                    out=ang[0:rows, :], in0=qf[0:rows, :], scalar=-2 * PI, in1=ang[0:rows, :], op0=ALU.mult, op1=ALU.add),
                    reads=[P.R("p_qf"), P.R("p_ang")], writes=[P.R("p_ang")])
                P.op("dve", lambda e, rows=rows: e.tensor_scalar(
                    out=qf[0:rows, :], in0=ang[0:rows, :], scalar1=-PI, scalar2=1e6, op0=ALU.add, op1=ALU.mult),
                    reads=[P.R("p_ang")], writes=[P.R("p_qf")])
                P.op("dve", lambda e, rows=rows: e.tensor_scalar(
                    out=qf[0:rows, :], in0=qf[0:rows, :], scalar1=0.0, scalar2=1.0, op0=ALU.max, op1=ALU.min),
                    reads=[P.R("p_qf")], writes=[P.R("p_qf")])
                P.op("dve", lambda e, rows=rows: e.scalar_tensor_tensor(
                    out=ang[0:rows, :], in0=qf[0:rows, :], scalar=-2 * PI, in1=ang[0:rows, :], op0=ALU.mult, op1=ALU.add),
                    reads=[P.R("p_qf"), P.R("p_ang")], writes=[P.R("p_ang")])
                P.op("dve", lambda e, rows=rows: e.tensor_scalar(
                    out=ang[0:rows, :], in0=ang[0:rows, :], scalar1=-3.14159, scalar2=3.14159, op0=ALU.max, op1=ALU.min),
                    reads=[P.R("p_ang")], writes=[P.R("p_ang")])
                P.op("act", lambda e, rows=rows, tab=tab: e.activation(
                    out=tab[0:rows, :], in_=ang[0:rows, :], func=AF.Sin),
                    reads=[P.R("p_ang")], writes=[P.R(res)])

        def simple_group(c0, ncols, out_dram, kind, bias_c0=None):
            for ch in range(ncols // 512):
                og, ogr = ostage()

                def cb(j, ps, pr, rows, og=og, ogr=ogr, ch=ch):
                    if kind == "silu":
                        P.op("act", lambda e: e.activation(out=og[:, j, :], in_=ps[:], func=AF.Silu),
                             reads=[pr], writes=[ogr])
                    elif kind == "copy":
                        P.op("act", lambda e: e.copy(out=og[:, j, :], in_=ps[:]), reads=[pr], writes=[ogr])
                    elif kind == "sigb":
                        gcol = bias_c0 + ch * 4 + j
                        P.op("act", lambda e: e.activation(out=og[:, j, :], in_=ps[:], func=AF.Sigmoid,
                                                           bias=gb[:, gcol:gcol + 1]),
                             reads=[pr, P.R("p_gb")], writes=[ogr])
                fm_tiles(c0 + ch * 512, 512, cb)
                P.dma("sp", out_dram[ch * 512:(ch + 1) * 512, ts].rearrange("(k p) t -> p k t", p=128), og[:],
                      reads=[ogr], writes=[P.R("p_out")])

        simple_group(0, 1024, io["zsT"], "silu")
        simple_group(1024, 1536, io["xbcT"], "copy")
        simple_group(6992, 1024, io["rgsT"], "silu")
        simple_group(8016, 6144, io["gT"], "sigb", bias_c0=0)

        def cb_dt(j, ps, pr, rows):
            P.op("act", lambda e: e.copy(out=dtst[:], in_=ps[0:16, :]), reads=[pr], writes=[P.R("p_dtst")])
            P.dma("sp", io["dtT"][:, ts], dtst[:], reads=[P.R("p_dtst")], writes=[P.R("p_out")])
        fm_tiles(2560, 16, cb_dt)

        def cb_dttm(sub, ps, pr):
            i = st["o32"] % 4
            st["o32"] += 1
            P.op("act", lambda e: e.copy(out=ost32[i][:], in_=ps[:, 0:16]), reads=[pr], writes=[P.R("p_ost32", i)])
            P.dma("sp", io["dttm"][b * TB + sub * 128:b * TB + (sub + 1) * 128, :], ost32[i][:],
                  reads=[P.R("p_ost32", i)], writes=[P.R("p_out")])
        tm_tiles(2560, 16, cb_dttm)

        for ch in range(2):
            def cb_rv2(sub, ps, pr, ch=ch):
                og, ogr = ostage()
                P.op("act", lambda e: e.copy(out=og[:, 0, :], in_=ps[:]), reads=[pr], writes=[ogr])
                P.dma("sp", io["rvtm"][b * TB + sub * 128:b * TB + (sub + 1) * 128, ch * 512:(ch + 1) * 512],
                      og[:, 0, :], reads=[ogr], writes=[P.R("p_out")])
            tm_tiles(5968 + ch * 512, 512, cb_rv2)

        for (c0, out_dram) in ((3920, io["rqT"]), (4944, io["rkT"])):
            for ch in range(2):
                og, ogr = ostage()
                held = {}

                def cb_r(j, ps, pr, rows, og=og, ogr=ogr, held=held):
                    tt, tr = tmp()
                    P.op("act", lambda e: e.copy(out=tt[:], in_=ps[:]), reads=[pr], writes=[tr])
                    held[j] = (tt, tr)
                    if j % 2 == 1:
                        x1, r1 = held[j - 1]
                        x2, r2 = held[j]
                        a, ar = tmp()
                        b_, br = tmp()
                        P.op("dve", lambda e: e.tensor_tensor(out=a[:], in0=x1[:], in1=cosr[:], op=ALU.mult),
                             reads=[r1, P.R("p_cosr")], writes=[ar])
                        P.op("dve", lambda e: e.tensor_tensor(out=b_[:], in0=x2[:], in1=sinr[:], op=ALU.mult),
                             reads=[r2, P.R("p_sinr")], writes=[br])
                        P.op("dve", lambda e: e.tensor_tensor(out=og[:, j - 1, :], in0=a[:], in1=b_[:], op=ALU.subtract),
                             reads=[ar, br], writes=[ogr])
                        P.op("dve", lambda e: e.tensor_tensor(out=a[:], in0=x1[:], in1=sinr[:], op=ALU.mult),
                             reads=[r1, P.R("p_sinr")], writes=[ar])
                        P.op("dve", lambda e: e.tensor_tensor(out=b_[:], in0=x2[:], in1=cosr[:], op=ALU.mult),
                             reads=[r2, P.R("p_cosr")], writes=[br])
                        P.op("dve", lambda e: e.tensor_tensor(out=og[:, j, :], in0=a[:], in1=b_[:], op=ALU.add),
                             reads=[ar, br], writes=[ogr])
                fm_tiles(c0 + ch * 512, 512, cb_r)
                P.dma("sp", out_dram[ch * 512:(ch + 1) * 512, ts].rearrange("(k p) t -> p k t", p=128), og[:],
                      reads=[ogr], writes=[P.R("p_out")])

        def lat_norm(c0, ntile, dst32, dstn, nw, nres, Dn, tagr):
            def cb(j, ps, pr, rows, base=0):
                P.op("act", lambda e: e.copy(out=dst32[:, base + j, :], in_=ps[:]), reads=[pr], writes=[P.R("p_x32")])
            done = 0
            while done < ntile:
                n = min(4, ntile - done)
                fm_tiles(c0 + done * 128, n * 128, lambda j, ps, pr, rows, base=done: cb(j, ps, pr, rows, base))
                done += n
            ps, pr = PS.get()
            for k in range(ntile):
                s_ = sq[k % 2]
                P.op("act", lambda e, k=k, s_=s_: e.activation(out=s_[:], in_=dst32[:, k, :], func=AF.Square),
                     reads=[P.R("p_x32")], writes=[P.R("p_sq", k % 2)])
                P.op("pe", lambda e, k=k, s_=s_, ps=ps: e.matmul(ps[:], lhsT=C.ones_bf[:], rhs=s_[:],
                                                               start=(k == 0), stop=(k == ntile - 1)),
                     reads=[P.R("p_sq", k % 2), P.R("ones_bf")], writes=[pr])
            rms_rstd(P, C, "p", ps[:], pr, Dn, hrstd[:], P.R("p_hrstd"))
            for k in range(ntile):
                P.op("dve", lambda e, k=k: e.scalar_tensor_tensor(
                    out=dstn[:, k, :], in0=dst32[:, k, :], scalar=nw[:, k:k + 1], in1=hrstd[:],
                    op0=ALU.mult, op1=ALU.mult),
                    reads=[P.R("p_x32"), nres, P.R("p_hrstd")], writes=[P.R(tagr + "n")])

        lat_norm(2576, 6, lat32, latn, qan, P.R("p_qan"), 768, "p_lat")
        lat_norm(3344, 4, ckv32, ckvn, kvan, P.R("p_kvan"), 512, "p_ckv")

        def rope64(src, sres, wcolap, wres, dst_f32, dres):
            P.op("dve", lambda e: e.tensor_scalar(out=xr[:], in0=src, scalar1=wcolap, scalar2=None, op0=ALU.mult),
                 reads=[sres, wres], writes=[P.R("p_xr")])
            ps, pr = PS.get()
            P.op("pe", lambda e, ps=ps: e.matmul(ps[0:64, :], lhsT=C.rmat[:], rhs=xr[:], start=True, stop=True),
                 reads=[P.R("p_xr"), P.R("c_rmat")], writes=[pr])
            a, ar = tmp()
            P.op("dve", lambda e: e.tensor_tensor(out=a[0:64, :], in0=xr[:], in1=cosm[:], op=ALU.mult),
                 reads=[P.R("p_xr"), P.R("p_cosm")], writes=[ar])
            b_, br = tmp()
            P.op("dve", lambda e, ps=ps: e.tensor_tensor(out=b_[0:64, :], in0=ps[0:64, :], in1=sinm[:], op=ALU.mult),
                 reads=[pr, P.R("p_sinm")], writes=[br])
            P.op("dve", lambda e: e.tensor_tensor(out=dst_f32, in0=a[0:64, :], in1=b_[0:64, :], op=ALU.add),
                 reads=[ar, br], writes=[dres])

        def cb_kpe(j, ps, pr, rows):
            P.op("act", lambda e: e.copy(out=kpe32[:], in_=ps[0:64, :]), reads=[pr], writes=[P.R("p_kpe32")])
        fm_tiles(3856, 64, cb_kpe)
        rope64(kpe32[:], P.R("p_kpe32"), nrm[0:64, 3:4], P.R("p_nrm"), kbase[:], P.R("p_kbase"))
        def do_qk(is_q, src, srcr, nk, wn, wnr, wr_, wrr, ncol, rcol, outn, outr):
            for hg in range(2):
                ogn, ognr = ostage()
                ogr_, ogrr = ostage()
                for hh in range(4):
                    h = hg * 4 + hh
                    psn, prn = PS.get()
                    for k in range(nk):
                        P.op("pe", lambda e, k=k, h=h, psn=psn: e.matmul(
                            psn[:], lhsT=wn[:, k, h * 128:(h + 1) * 128], rhs=src[:, k, :],
                            start=(k == 0), stop=(k == nk - 1)), reads=[wnr, srcr], writes=[prn])
                    if is_q:
                        psr, prr = PS.get()
                        for k in range(nk):
                            P.op("pe", lambda e, k=k, h=h, psr=psr: e.matmul(
                                psr[0:64, :], lhsT=wr_[:, k, h * 64:(h + 1) * 64], rhs=src[:, k, :],
                                start=(k == 0), stop=(k == nk - 1)), reads=[wrr, srcr], writes=[prr])
                        rsrc, rres = tmp()
                        P.op("act", lambda e, psr=psr, rsrc=rsrc: e.copy(out=rsrc[0:64, :], in_=psr[0:64, :]),
                             reads=[prr], writes=[rres])
                        rsrc_ap = rsrc[0:64, :]
                    else:
                        rsrc_ap, rres = kpe32[:], P.R("p_kpe32")
                    pss, prs = PS.get()
                    P.op("act", lambda e, psn=psn: e.activation(out=sq[1][:], in_=psn[:], func=AF.Square),
                         reads=[prn], writes=[P.R("p_sq", 1)])
                    P.op("pe", lambda e, pss=pss: e.matmul(pss[:], lhsT=C.ones_bf[:], rhs=sq[1][:], start=True, stop=False),
                         reads=[P.R("p_sq", 1), P.R("ones_bf")], writes=[prs])
                    P.op("act", lambda e, rsrc_ap=rsrc_ap: e.activation(out=sq[0][0:64, :], in_=rsrc_ap, func=AF.Square),
                         reads=[rres], writes=[P.R("p_sq", 0)])
                    P.op("pe", lambda e, pss=pss: e.matmul(pss[:], lhsT=C.ones_bf[0:64, :], rhs=sq[0][0:64, :],
                                                           start=False, stop=True),
                         reads=[P.R("p_sq", 0), P.R("ones_bf")], writes=[prs])
                    rms_rstd(P, C, "p", pss[:], prs, 192, hrstd[:], P.R("p_hrstd"))
                    P.op("dve", lambda e, psn=psn, hh=hh, ogn=ogn: e.scalar_tensor_tensor(
                        out=ogn[:, hh, :], in0=psn[:], scalar=nrm[:, ncol:ncol + 1], in1=hrstd[:],
                        op0=ALU.mult, op1=ALU.mult), reads=[prn, P.R("p_nrm"), P.R("p_hrstd")], writes=[ognr])
                    if is_q:
                        rb, rbr = tmp()
                        rope64(rsrc_ap, rres, nrm[0:64, rcol:rcol + 1], P.R("p_nrm"), rb[0:64, :], rbr)
                        rb_ap = rb[0:64, :]
                    else:
                        rb_ap, rbr = kbase[:], P.R("p_kbase")
                    P.op("dve", lambda e, rb_ap=rb_ap, hh=hh, ogr_=ogr_: e.tensor_tensor(
                        out=ogr_[0:64, hh, :], in0=rb_ap, in1=hrstd[0:64, :], op=ALU.mult),
                        reads=[rbr, P.R("p_hrstd")], writes=[ogrr])
                P.dma("sp", outn[hg * 512:(hg + 1) * 512, ts].rearrange("(k p) t -> p k t", p=128), ogn[:],
                      reads=[ognr], writes=[P.R("p_out")])
                P.dma("sp", outr[hg * 256:(hg + 1) * 256, ts].rearrange("(k p) t -> p k t", p=64), ogr_[0:64, :, :],
                      reads=[ogrr], writes=[P.R("p_out")])
        do_qk(True, latn, P.R("p_latn"), 6, wqn, P.R("p_wqn"), wqr, P.R("p_wqr"), 0, 2, io["qnT"], io["qrT"])
        do_qk(False, ckvn, P.R("p_ckvn"), 4, wkk, P.R("p_wkk"), None, None, 1, 3, io["knT"], io["krT"])
        for sub in range(4):
            for ch in range(2):
                ps, pr = PS.get()
                for k in range(4):
                    P.op("pe", lambda e, k=k, sub=sub, ch=ch, ps=ps: e.matmul(
                        ps[:], lhsT=ckvn[:, k, sub * 128:(sub + 1) * 128], rhs=wkv[:, k, ch * 512:(ch + 1) * 512],
                        start=(k == 0), stop=(k == 3)), reads=[P.R("p_ckvn"), P.R("p_wkv")], writes=[pr])
                og, ogr = ostage()
                P.op("act", lambda e, og=og, ps=ps: e.copy(out=og[:, 0, :], in_=ps[:]), reads=[pr], writes=[ogr])
                P.dma("sp", io["vtm"][b * TB + sub * 128:b * TB + (sub + 1) * 128, ch * 512:(ch + 1) * 512],
                      og[:, 0, :], reads=[ogr], writes=[P.R("p_out")])


LEVEL = 99


def ssd_stage(P, C, PS, psbf, io, S):
    nc = P.nc
    NT = S // 128
    xc = P.sb("s_xc", [128, 6, S], BF16)
    zs = P.sb("s_zs", [128, 4, S], BF16)
    raw = [P.sb("s_raw%d" % i, [128, S + 3], BF16) for i in range(2)]
    cacc = [P.sb("s_cacc%d" % i, [128, 1024], F32) for i in range(2)]
    cw = P.sb("s_cw", [128, 6, 4], F32)
    cbias = P.sb("s_cb", [128, 6], F32)
    dcol = P.sb("s_dcol", [128, 4], F32)
    ssmw = P.sb("s_ssmw", [128, 4], F32)
    dtb_c = P.sb("s_dtbc", [8, 1], F32)
    a_c = P.sb("s_ac", [8, 1], F32)
    dtb_r = P.sb("s_dtbr", [128, 8], F32)
    a_r = P.sb("s_ar", [128, 8], F32)
    one_c = P.sb("s_one", [128, 1], F32)
    dtT = P.sb("s_dtT", [8, S], F32)
    adtT = P.sb("s_adtT", [8, S], F32)
    dttm = P.sb("s_dttm", [128, NT, 8], F32)
    adttm = P.sb("s_adttm", [128, NT, 8], F32)
    acsT = [P.sb("s_acsT%d" % i, [8, 128], F32) for i in range(2)]
    neglT = [P.sb("s_neglT%d" % i, [8, 128], F32) for i in range(2)]
    sm = [P.sb("s_sm%d" % i, [128, 32], F32) for i in range(2)]
    cb_sb = [P.sb("s_cbsb%d" % i, [128, 128], F32) for i in range(2)]
    dec = [P.sb("s_dec%d" % i, [128, 128], F32) for i in range(3)]
    esb = [P.sb("s_esb%d" % i, [128, 128], F32) for i in range(2)]
    MT = [P.sb("s_MT%d" % i, [128, 128], BF16) for i in range(3)]
    Cs = [P.sb("s_Cs%d" % i, [128, 128], BF16) for i in range(3)]
    xpad = [P.sb("s_xpad%d" % i, [128, 8, 128], BF16) for i in range(2)]
    xdd = [P.sb("s_xdd%d" % i, [128, 512], BF16) for i in range(2)]
    btm = [P.sb("s_btm%d" % i, [128, 128], BF16) for i in range(2)]
    prev32 = P.sb("s_prev32", [128, 512], F32)
    ptmp = P.sb("s_ptmp", [128, 512], F32)
    ppad = [P.sb("s_ppad%d" % i, [128, 8, 128], BF16) for i in range(2)]
    ygt = [P.sb("s_ygt%d" % i, [128, 4, 128], F32) for i in range(2)]
    ysq = [P.sb("s_ysq%d" % i, [128, 128], BF16) for i in range(2)]
    yrstd = P.sb("s_yrstd", [128, 128], F32)
    yst = [P.sb("s_yst%d" % i, [128, 4, 128], BF16) for i in range(2)]

    for (t, key, res) in ((cw, "convw", "s_cw"), (cbias, "convb", "s_cb"), (dcol, "dskip", "s_dcol"),
                          (ssmw, "ssmw", "s_ssmw"), (dtb_c, "dtb_c", "s_dtbc"), (a_c, "alog_c", "s_ac"),
                          (dtb_r, "dtb_r", "s_dtbr"), (a_r, "alog_r", "s_ar")):
        P.dma("sp", t[:], io[key], writes=[P.R(res)])
    P.op("dve", lambda e: e.memset(one_c[:], 1.0), writes=[P.R("s_one")])
    for i in range(2):
        P.op("pool", lambda e, i=i: e.memset(xpad[i][:], 0.0), writes=[P.R("s_xpad", i)])
        P.op("pool", lambda e, i=i: e.memset(ppad[i][:], 0.0), writes=[P.R("s_ppad", i)])
        P.op("dve", lambda e, i=i: e.memset(raw[i][:, 0:3], 0.0), writes=[P.R("s_rawz", i)])
    for (t, res, rows) in ((a_c, "s_ac", 8), (a_r, "s_ar", 128)):
        P.op("act", lambda e, t=t: e.activation(out=t[:], in_=t[:], func=AF.Exp), reads=[P.R(res)], writes=[P.R(res)])
        P.op("dve", lambda e, t=t: e.tensor_scalar(out=t[:], in0=t[:], scalar1=-1.0, scalar2=None, op0=ALU.mult),
             reads=[P.R(res)], writes=[P.R(res)])
    P.dma("sp", zs[:], io["zsT"].rearrange("(k p) t -> p k t", p=128), writes=[P.R("s_zs")])
    srcs = [io["xT"][i * 128:(i + 1) * 128, :] for i in range(4)] + [io["BT"], io["CT"]]
    for ti in range(6):
        rb = raw[ti % 2]
        rres = P.R("s_raw", ti % 2)
        P.dma("sp", rb[:, 3:3 + S], srcs[ti], reads=[P.R("s_rawz", ti % 2)], writes=[rres])
        for c0 in range(0, S, 1024):
            n = min(1024, S - c0)
            ca = cacc[(c0 // 1024) % 2]
            car = P.R("s_cacc", (c0 // 1024) % 2)
            P.op("dve", lambda e, ca=ca, rb=rb, c0=c0, n=n, ti=ti: e.tensor_scalar(
                out=ca[:, 0:n], in0=rb[:, c0:c0 + n], scalar1=cw[:, ti, 0:1], scalar2=None, op0=ALU.mult),
                reads=[rres, P.R("s_cw")], writes=[car])
            for j in range(1, 4):
                P.op("dve", lambda e, ca=ca, rb=rb, c0=c0, n=n, ti=ti, j=j: e.scalar_tensor_tensor(
                    out=ca[:, 0:n], in0=rb[:, c0 + j:c0 + j + n], scalar=cw[:, ti, j:j + 1], in1=ca[:, 0:n],
                    op0=ALU.mult, op1=ALU.add), reads=[rres, P.R("s_cw"), car], writes=[car])
            P.op("act", lambda e, ca=ca, c0=c0, n=n, ti=ti: e.activation(
                out=xc[:, ti, c0:c0 + n], in_=ca[:, 0:n], func=AF.Silu, bias=cbias[:, ti:ti + 1]),
                reads=[car, P.R("s_cb")], writes=[P.R("s_xc", ti)])
    P.dma("sp", dtT[:], io["dtT"], writes=[P.R("s_dtT")])
    P.dma("sp", dttm[:], io["dttm"].rearrange("(n p) h -> p n h", p=128), writes=[P.R("s_dttm")])
    P.op("act", lambda e: e.activation(out=dtT[:], in_=dtT[:], func=AF.Exp, bias=dtb_c[:, 0:1]),
         reads=[P.R("s_dtT"), P.R("s_dtbc")], writes=[P.R("s_dtT")])
    P.op("act", lambda e: e.activation(out=dtT[:], in_=dtT[:], func=AF.Ln, bias=one_c[0:8, 0:1]),
         reads=[P.R("s_dtT"), P.R("s_one")], writes=[P.R("s_dtT")])
    P.op("act", lambda e: e.activation(out=adtT[:], in_=dtT[:], func=AF.Ln),
         reads=[P.R("s_dtT")], writes=[P.R("s_adtT")])
    P.op("dve", lambda e: e.tensor_tensor(out=dttm[:], in0=dttm[:], in1=dtb_r[:].unsqueeze(1).to_broadcast([128, NT, 8]),
                                          op=ALU.add), reads=[P.R("s_dttm"), P.R("s_dtbr")], writes=[P.R("s_dttm")])
    P.op("act", lambda e: e.activation(out=dttm[:], in_=dttm[:], func=AF.Exp), reads=[P.R("s_dttm")], writes=[P.R("s_dttm")])
    P.op("act", lambda e: e.activation(out=dttm[:], in_=dttm[:], func=AF.Ln, bias=one_c[:, 0:1]),
         reads=[P.R("s_dttm"), P.R("s_one")], writes=[P.R("s_dttm")])
    P.op("dve", lambda e: e.tensor_tensor(out=adttm[:], in0=dttm[:], in1=a_r[:].unsqueeze(1).to_broadcast([128, NT, 8]),
                                          op=ALU.mult), reads=[P.R("s_dttm"), P.R("s_ar")], writes=[P.R("s_adttm")])
    if LEVEL <= 1:
        return
    yv = io["yT"].rearrange("(k p) t -> p k t", p=128)
    cnt = {"dec": 0, "mt": 0, "cs": 0, "e": 0}
    for c in range(NT):
        cs_ = slice(c * 128, (c + 1) * 128)
        b2 = c % 2
        ps1, pr1 = PS.get()
        P.op("pe", lambda e, ps1=ps1, c=c: e.matmul(ps1[0:8, 0:128], lhsT=adttm[:, c, :], rhs=C.tri[:], start=True, stop=True),
             reads=[P.R("s_adttm"), P.R("c_tri")], writes=[pr1])
        ps2, pr2 = PS.get()
        P.op("pe", lambda e, ps2=ps2, c=c: e.matmul(ps2[:, 0:8], lhsT=C.tri[:], rhs=adttm[:, c, :], start=True, stop=True),
             reads=[P.R("s_adttm"), P.R("c_tri")], writes=[pr2])
        P.op("pe", lambda e, ps2=ps2, c=c: e.matmul(ps2[:, 8:16], lhsT=C.ones32[:], rhs=adttm[:, c, :], start=True, stop=True),
             reads=[P.R("s_adttm"), P.R("c_ones32")], writes=[pr2])
        aT, aTr = acsT[b2], P.R("s_acsT", b2)
        nT, nTr = neglT[b2], P.R("s_neglT", b2)
        P.op("act", lambda e, ps1=ps1, aT=aT: e.copy(out=aT[:], in_=ps1[0:8, 0:128]), reads=[pr1], writes=[aTr])
        P.op("dve", lambda e, aT=aT, nT=nT, cs_=cs_: e.tensor_tensor(out=nT[:], in0=adtT[:, cs_], in1=aT[:], op=ALU.subtract),
             reads=[P.R("s_adtT"), aTr], writes=[nTr])
        s_, sr = sm[b2], P.R("s_sm", b2)
        P.op("act", lambda e, ps2=ps2, s_=s_: e.copy(out=s_[:, 0:8], in_=ps2[:, 8:16]), reads=[pr2], writes=[sr])
        P.op("act", lambda e, ps2=ps2, s_=s_: e.activation(out=s_[:, 8:16], in_=ps2[:, 8:16], func=AF.Exp),
             reads=[pr2], writes=[sr])
        P.op("dve", lambda e, ps2=ps2, s_=s_: e.tensor_tensor(out=s_[:, 16:24], in0=s_[:, 0:8], in1=ps2[:, 0:8], op=ALU.subtract),
             reads=[pr2, sr], writes=[sr])
        P.op("act", lambda e, s_=s_: e.activation(out=s_[:, 16:24], in_=s_[:, 16:24], func=AF.Exp), reads=[sr], writes=[sr])
        P.op("dve", lambda e, s_=s_, c=c: e.tensor_tensor(out=s_[:, 24:32], in0=s_[:, 16:24], in1=dttm[:, c, :], op=ALU.mult),
             reads=[sr, P.R("s_dttm")], writes=[sr])
        if LEVEL == 20:
            continue
        ps3, pr3 = PS.get()
        P.op("pe", lambda e, ps3=ps3, cs_=cs_: e.matmul(ps3[:, 0:128], lhsT=xc[:, 4, cs_], rhs=xc[:, 5, cs_], start=True, stop=True),
             reads=[P.R("s_xc", 4), P.R("s_xc", 5)], writes=[pr3])
        cbs, cbr = cb_sb[b2], P.R("s_cbsb", b2)
        P.op("act", lambda e, ps3=ps3, cbs=cbs: e.copy(out=cbs[:], in_=ps3[:, 0:128]), reads=[pr3], writes=[cbr])
        if LEVEL == 21:
            continue
        pb, pbr = PS.get()
        pbB, pbBr = PS.get()
        for ti in range(4):
            P.op("pe", lambda e, pb=pb, ti=ti, cs_=cs_: e.matmul(pb[:, ti * 128:(ti + 1) * 128], lhsT=xc[:, ti, cs_], rhs=C.ident[:],
                                                                start=True, stop=True),
                 reads=[P.R("s_xc", ti), P.R("c_ident")], writes=[pbr])
        P.op("pe", lambda e, pbB=pbB, cs_=cs_: e.matmul(pbB[:, 0:128], lhsT=xc[:, 4, cs_], rhs=C.ident[:], start=True, stop=True),
             reads=[P.R("s_xc", 4), P.R("c_ident")], writes=[pbBr])
        if LEVEL == 24:
            continue
        xp, xpr = xpad[b2], P.R("s_xpad", b2)
        pv = pb[:, 0:512].rearrange("p (a b c) -> p a b c", a=4, b=2, c=64)
        xpv = xp[:].rearrange("p (a b) c -> p a b c", b=2)
        P.op("act", lambda e, pv=pv, xpv=xpv: e.copy(out=xpv[:, :, 0, 0:64], in_=pv[:, :, 0, :]), reads=[pbr], writes=[xpr])
        P.op("act", lambda e, pv=pv, xpv=xpv: e.copy(out=xpv[:, :, 1, 64:128], in_=pv[:, :, 1, :]), reads=[pbr], writes=[xpr])
        if LEVEL == 22:
            continue
        xd, xdr = xdd[b2], P.R("s_xdd", b2)
        for h8 in range(8 if LEVEL != 26 else 0):
            P.op("act", lambda e, pb=pb, xd=xd, s_=s_, h8=h8: e.activation(
                out=xd[:, h8 * 64:(h8 + 1) * 64], in_=pb[:, h8 * 64:(h8 + 1) * 64], func=AF.Copy, scale=s_[:, 24 + h8:25 + h8]),
                reads=[pbr, sr], writes=[xdr])
        if LEVEL == 25:
            continue
        bt, btr = btm[b2], P.R("s_btm", b2)
        P.op("act", lambda e, pbB=pbB, bt=bt: e.copy(out=bt[:], in_=pbB[:, 0:128]), reads=[pbBr], writes=[btr])
        if LEVEL <= 2:
            continue
        pp, ppr = ppad[b2], P.R("s_ppad", b2)
        yg, ygr = ygt[b2], P.R("s_ygt", b2)
        for pair in range(4):
            psy, pry = PS.get()
            for hh in range(2):
                h = pair * 2 + hh
                pseg, prs = PS.get()
                P.op("pe", lambda e, pseg=pseg, h=h, aT=aT: e.matmul(pseg[:, 0:128], lhsT=C.sel[:, h, :], rhs=aT[:], start=True, stop=False),
                     reads=[P.R("c_sel"), aTr], writes=[prs])
                P.op("pe", lambda e, pseg=pseg, h=h, nT=nT: e.matmul(pseg[:, 0:128], lhsT=nT[:], rhs=C.sel[:, h, :], start=False, stop=False),
                     reads=[P.R("c_sel"), nTr], writes=[prs])
                P.op("pe", lambda e, pseg=pseg: e.matmul(pseg[:, 0:128], lhsT=C.ident[:], rhs=C.negtri[:], start=False, stop=True),
                     reads=[P.R("c_ident"), P.R("c_negtri")], writes=[prs])
                i = cnt["dec"] % 3
                cnt["dec"] += 1
                dc, dcr = dec[i], P.R("s_dec", i)
                P.op("act", lambda e, pseg=pseg, dc=dc: e.activation(out=dc[:], in_=pseg[:, 0:128], func=AF.Exp),
                     reads=[prs], writes=[dcr])
                mt, mtr = MT[i], P.R("s_MT", i)
                P.op("dve", lambda e, dc=dc, mt=mt, cbs=cbs: e.tensor_tensor(out=mt[:], in0=dc[:], in1=cbs[:], op=ALU.mult),
                     reads=[dcr, cbr], writes=[mtr])
                last = (hh == 1) and (c == 0)
                P.op("pe", lambda e, psy=psy, xp=xp, h=h, mt=mt, hh=hh, last=last: e.matmul(
                    psy[:, 0:128], lhsT=xp[:, h, :], rhs=mt[:], start=(hh == 0), stop=last),
                    reads=[xpr, mtr], writes=[pry])
                if c > 0:
                    pe_, per = PS.get()
                    P.op("pe", lambda e, pe_=pe_, h=h, aT=aT: e.matmul(pe_[:, 0:128], lhsT=C.sel[:, h, :], rhs=aT[:], start=True, stop=True),
                         reads=[P.R("c_sel"), aTr], writes=[per])
                    ie = cnt["e"] % 2
                    cnt["e"] += 1
                    es, esr = esb[ie], P.R("s_esb", ie)
                    P.op("act", lambda e, pe_=pe_, es=es: e.activation(out=es[:], in_=pe_[:, 0:128], func=AF.Exp),
                         reads=[per], writes=[esr])
                    cs2, csr = Cs[i], P.R("s_Cs", i)
                    P.op("dve", lambda e, es=es, cs2=cs2, cs_=cs_: e.tensor_tensor(out=cs2[:], in0=xc[:, 5, cs_], in1=es[:], op=ALU.mult),
                         reads=[esr, P.R("s_xc", 5)], writes=[csr])
                    P.op("pe", lambda e, psy=psy, pp=pp, h=h, cs2=cs2, hh=hh: e.matmul(
                        psy[:, 0:128], lhsT=pp[:, h, :], rhs=cs2[:], start=False, stop=(hh == 1)),
                        reads=[ppr, csr], writes=[pry])
            P.op("dve", lambda e, psy=psy, yg=yg, pair=pair, cs_=cs_: e.scalar_tensor_tensor(
                out=yg[:, pair, :], in0=xc[:, pair, cs_], scalar=dcol[:, pair:pair + 1], in1=psy[:, 0:128],
                op0=ALU.mult, op1=ALU.add), reads=[pry, P.R("s_xc", pair), P.R("s_dcol")], writes=[ygr])
            P.op("pool", lambda e, yg=yg, pair=pair, cs_=cs_: e.tensor_tensor(
                out=yg[:, pair, :], in0=yg[:, pair, :], in1=zs[:, pair, cs_], op=ALU.mult),
                reads=[ygr, P.R("s_zs")], writes=[ygr])
        if LEVEL <= 3:
            continue
        pss, prss = PS.get()
        for pair in range(4):
            q_, qr = ysq[pair % 2], P.R("s_ysq", pair % 2)
            P.op("act", lambda e, q_=q_, yg=yg, pair=pair: e.activation(out=q_[:], in_=yg[:, pair, :], func=AF.Square),
                 reads=[ygr], writes=[qr])
            P.op("pe", lambda e, q_=q_, pss=pss, pair=pair: e.matmul(pss[:, 0:128], lhsT=C.ones_bf[:], rhs=q_[:],
                                                                  start=(pair == 0), stop=(pair == 3)),
                 reads=[qr, P.R("ones_bf")], writes=[prss])
        rms_rstd(P, C, "s", pss[:, 0:128], prss, 512, yrstd[:], P.R("s_yrstd"))
        ys, ysr = yst[b2], P.R("s_yst", b2)
        for pair in range(4):
            P.op("dve", lambda e, ys=ys, yg=yg, pair=pair: e.scalar_tensor_tensor(
                out=ys[:, pair, :], in0=yg[:, pair, :], scalar=ssmw[:, pair:pair + 1], in1=yrstd[:],
                op0=ALU.mult, op1=ALU.mult), reads=[ygr, P.R("s_ssmw"), P.R("s_yrstd")], writes=[ysr])
        P.dma("sp", yv[:, :, cs_], ys[:], reads=[ysr], writes=[P.R("s_out")])
        if c < NT - 1:
            pst, prst = PS.get()
            P.op("pe", lambda e, pst=pst, bt=bt, xd=xd: e.matmul(pst[:], lhsT=bt[:], rhs=xd[:], start=True, stop=True),
                 reads=[btr, xdr], writes=[prst])
            if c == 0:
                P.op("dve", lambda e, pst=pst: e.tensor_copy(out=prev32[:], in_=pst[:]), reads=[prst], writes=[P.R("s_prev32")])
            else:
                for h8 in range(8):
                    P.op("dve", lambda e, s_=s_, pst=pst, h8=h8: e.scalar_tensor_tensor(
                        out=prev32[:, h8 * 64:(h8 + 1) * 64], in0=prev32[:, h8 * 64:(h8 + 1) * 64], scalar=s_[:, 8 + h8:9 + h8],
                        in1=pst[:, h8 * 64:(h8 + 1) * 64], op0=ALU.mult, op1=ALU.add),
                        reads=[P.R("s_prev32"), sr, prst], writes=[P.R("s_prev32")])
            npp, nppr = ppad[(c + 1) % 2], P.R("s_ppad", (c + 1) % 2)
            p4 = prev32[:].rearrange("p (a b c) -> p a b c", a=4, b=2, c=64)
            nv = npp[:].rearrange("p (a b) c -> p a b c", b=2)
            P.op("act", lambda e, p4=p4, nv=nv: e.copy(out=nv[:, :, 0, 0:64], in_=p4[:, :, 0, :]), reads=[P.R("s_prev32")], writes=[nppr])
            P.op("act", lambda e, p4=p4, nv=nv: e.copy(out=nv[:, :, 1, 64:128], in_=p4[:, :, 1, :]), reads=[P.R("s_prev32")], writes=[nppr])


def ret_stage(P, C, PS, psbf, io, S):
    NT = S // 128
    rq = P.sb("r_rq", [128, 4, S], BF16)
    rk = P.sb("r_rk", [128, 4, S], BF16)
    rg = P.sb("r_rg", [128, 4, S], BF16)
    rv = P.sb("r_rv", [128, NT, 512], BF16)
    dm = P.sb("r_dm", [128, 2, 128], F32)
    qd = P.sb("r_qd", [128, 2, 128], F32)
    kd = P.sb("r_kd", [128, 2], F32)
    cd = P.sb("r_cd", [128, 2], F32)
    retw = P.sb("r_retw", [128, 4], F32)
    PT = [P.sb("r_PT%d" % i, [128, 128], BF16) for i in range(2)]
    kdt = [P.sb("r_kdt%d" % i, [128, 256], BF16) for i in range(2)]
    qs = [P.sb("r_qs%d" % i, [128, 2, 128], BF16) for i in range(2)]
    prev32 = P.sb("r_prev32", [128, 4, 256], F32)
    prevbf = [P.sb("r_prevbf%d" % i, [128, 4, 256], BF16) for i in range(2)]
    ysb = [P.sb("r_ysb%d" % i, [128, 2, 128], F32) for i in range(2)]
    ysq = [P.sb("r_ysq%d" % i, [128, 128], BF16) for i in range(2)]
    yrstd = P.sb("r_yrstd", [128, 128], F32)
    yst = [P.sb("r_yst%d" % i, [128, 4, 128], BF16) for i in range(2)]
    for (t, key, res) in ((dm, "ret_dm", "r_dm"), (qd, "ret_qd", "r_qd"), (kd, "ret_kd", "r_kd"),
                          (cd, "ret_cd", "r_cd"), (retw, "retw", "r_retw")):
        P.dma("sp", t[:], io[key], writes=[P.R(res)])
    P.dma("sp", rq[:], io["rqT"].rearrange("(k p) t -> p k t", p=128), writes=[P.R("r_rq")])
    P.dma("sp", rk[:], io["rkT"].rearrange("(k p) t -> p k t", p=128), writes=[P.R("r_rk")])
    P.dma("sp", rg[:], io["rgsT"].rearrange("(k p) t -> p k t", p=128), writes=[P.R("r_rg")])
    P.dma("sp", rv[:], io["rvtm"].rearrange("(n p) e -> p n e", p=128), writes=[P.R("r_rv")])
    yv = io["yT"].rearrange("(k p) t -> p k t", p=128)
    for c in range(NT):
        cs_ = slice(c * 128, (c + 1) * 128)
        b2 = c % 2
        ys_, ysr = yst[b2], P.R("r_yst", b2)
        pbn, pbnr = prevbf[(c + 1) % 2], P.R("r_prevbf", (c + 1) % 2)
        pbc, pbcr = prevbf[c % 2], P.R("r_prevbf", c % 2)
        for h in range(2):
            ps, pr = PS.get()
            for dt in range(2):
                P.op("pe", lambda e, ps=ps, h=h, dt=dt, cs_=cs_: e.matmul(
                    ps[:, 0:128], lhsT=rk[:, h * 2 + dt, cs_], rhs=rq[:, h * 2 + dt, cs_], start=(dt == 0), stop=(dt == 1)),
                    reads=[P.R("r_rk"), P.R("r_rq")], writes=[pr])
            pt, ptr = PT[h], P.R("r_PT", h)
            P.op("dve", lambda e, ps=ps, pt=pt, h=h: e.tensor_tensor(out=pt[:], in0=ps[:, 0:128], in1=dm[:, h, :], op=ALU.mult),
                 reads=[pr, P.R("r_dm")], writes=[ptr])
            if c > 0:
                q_, qr = qs[h], P.R("r_qs", h)
                for d2 in range(2):
                    P.op("pool", lambda e, q_=q_, h=h, cs_=cs_, d2=d2: e.tensor_tensor(
                        out=q_[:, d2, :], in0=rq[:, h * 2 + d2, cs_], in1=qd[:, h, :], op=ALU.mult),
                        reads=[P.R("r_rq"), P.R("r_qd")], writes=[qr])
            yb, ybr = ysb[h], P.R("r_ysb", h)
            psq, prq = PS.get()
            for et in range(2):
                py, pyr = PS.get()
                P.op("pe", lambda e, py=py, h=h, et=et, c=c, pt=pt: e.matmul(
                    py[:, 0:128], lhsT=rv[:, c, h * 256 + et * 128:h * 256 + (et + 1) * 128], rhs=pt[:],
                    start=True, stop=(c == 0)), reads=[P.R("r_rv"), ptr], writes=[pyr])
                if c > 0:
                    for dt in range(2):
                        P.op("pe", lambda e, py=py, h=h, et=et, dt=dt, q_=q_, pbc=pbc: e.matmul(
                            py[:, 0:128], lhsT=pbc[:, h * 2 + dt, et * 128:(et + 1) * 128], rhs=q_[:, dt, :],
                            start=False, stop=(dt == 1)), reads=[pbcr, qr], writes=[pyr])
                P.op("act", lambda e, py=py, yb=yb, et=et: e.copy(out=yb[:, et, :], in_=py[:, 0:128]), reads=[pyr], writes=[ybr])
                s_, sr = ysq[et], P.R("r_ysq", et)
                P.op("act", lambda e, py=py, s_=s_: e.activation(out=s_[:], in_=py[:, 0:128], func=AF.Square),
                     reads=[pyr], writes=[sr])
                P.op("pe", lambda e, psq=psq, s_=s_, et=et: e.matmul(psq[:, 0:128], lhsT=C.ones_bf[:], rhs=s_[:],
                                                                    start=(et == 0), stop=(et == 1)),
                     reads=[sr, P.R("ones_bf")], writes=[prq])
            rms_rstd(P, C, "r", psq[:, 0:128], prq, 256, yrstd[:], P.R("r_yrstd"))
            for et in range(2):
                P.op("dve", lambda e, yb=yb, et=et, h=h: e.scalar_tensor_tensor(
                    out=yb[:, et, :], in0=yb[:, et, :], scalar=retw[:, h * 2 + et:h * 2 + et + 1], in1=yrstd[:],
                    op0=ALU.mult, op1=ALU.mult), reads=[ybr, P.R("r_retw"), P.R("r_yrstd")], writes=[ybr])
                P.op("pool", lambda e, yb=yb, et=et, h=h, ys_=ys_, cs_=cs_: e.tensor_tensor(
                    out=ys_[:, h * 2 + et, :], in0=yb[:, et, :], in1=rg[:, h * 2 + et, cs_], op=ALU.mult),
                    reads=[ybr, P.R("r_rg")], writes=[ysr])
            if c < NT - 1:
                pb, pbr = PS.get()
                for dt in range(2):
                    P.op("pe", lambda e, pb=pb, h=h, dt=dt, cs_=cs_: e.matmul(
                        pb[:, dt * 128:(dt + 1) * 128], lhsT=rk[:, h * 2 + dt, cs_], rhs=C.ident[:], start=True, stop=True),
                        reads=[P.R("r_rk"), P.R("c_ident")], writes=[pbr])
                kt, ktr = kdt[h], P.R("r_kdt", h)
                P.op("act", lambda e, pb=pb, kt=kt, h=h: e.activation(
                    out=kt[:], in_=pb[:, 0:256], func=AF.Copy, scale=kd[:, h:h + 1]),
                    reads=[pbr, P.R("r_kd")], writes=[ktr])
                for dt in range(2):
                    pst, pstr = PS.get()
                    P.op("pe", lambda e, pst=pst, kt=kt, dt=dt, c=c, h=h: e.matmul(
                        pst[:, 0:256], lhsT=kt[:, dt * 128:(dt + 1) * 128], rhs=rv[:, c, h * 256:(h + 1) * 256],
                        start=True, stop=True), reads=[ktr, P.R("r_rv")], writes=[pstr])
                    pres = P.R("r_prev32", h, dt)
                    if c == 0:
                        P.op("dve", lambda e, pst=pst, h=h, dt=dt: e.tensor_copy(out=prev32[:, h * 2 + dt, :], in_=pst[:, 0:256]),
                             reads=[pstr], writes=[pres])
                    else:
                        P.op("dve", lambda e, pst=pst, h=h, dt=dt: e.scalar_tensor_tensor(
                            out=prev32[:, h * 2 + dt, :], in0=prev32[:, h * 2 + dt, :], scalar=cd[:, h:h + 1], in1=pst[:, 0:256],
                            op0=ALU.mult, op1=ALU.add), reads=[pstr, pres, P.R("r_cd")], writes=[pres])
                    P.op("act", lambda e, h=h, dt=dt, pbn=pbn: e.copy(out=pbn[:, h * 2 + dt, :], in_=prev32[:, h * 2 + dt, :]),
                         reads=[pres], writes=[pbnr])
        P.dma("sp", yv[:, :, cs_], ys_[:], reads=[ysr], writes=[P.R("r_out")])


def attn_stage(P, C, PS, io, S, NH=4):
    NQ = S // 512
    NT = S // 128
    scale = 192 ** -0.5
    qn = P.sb("a_qn", [128, NH, S], BF16)
    kn = P.sb("a_kn", [128, NH, S], BF16)
    qr = P.sb("a_qr", [64, NH, S], BF16)
    kr = P.sb("a_kr", [64, NH, S], BF16)
    v = P.sb("a_v", [128, NT, NH * 128], BF16)
    pT = [P.sb("a_pT%d" % i, [128, 512], BF16) for i in range(3)]
    rden = P.sb("a_rden", [128, 512], F32)
    ost = [P.sb("a_ost%d" % i, [128, 512], BF16) for i in range(2)]
    P.dma("sp", qn[:], io["qnT"].rearrange("(h p) t -> p h t", p=128), writes=[P.R("a_qn")])
    P.dma("sp", kn[:], io["knT"].rearrange("(h p) t -> p h t", p=128), writes=[P.R("a_kn")])
    P.dma("sp", qr[:], io["qrT"].rearrange("(h p) t -> p h t", p=64), writes=[P.R("a_qr")])
    P.dma("sp", kr[:], io["krT"].rearrange("(h p) t -> p h t", p=64), writes=[P.R("a_kr")])
    P.dma("sp", v[:], io["vtm"].rearrange("(n p) e -> p n e", p=128), writes=[P.R("a_v")])
    pi = 0
    oi = 0
    po_t = [P.ps("a_po%d" % i, [128, 512]) for i in range(2)]
    pd_t = [P.ps("a_pd%d" % i, [128, 512]) for i in range(2)]
    qi = 0
    for h in range(NH):
        for qb in range(NQ):
            qs_ = slice(qb * 512, (qb + 1) * 512)
            po, por = po_t[qi % 2], P.R("a_po", qi % 2)
            pd, pdr = pd_t[qi % 2], P.R("a_pd", qi % 2)
            qi += 1
            nkt = 4 * (qb + 1)
            for kt in range(nkt):
                ks_ = slice(kt * 128, (kt + 1) * 128)
                ps, pr = PS.get()
                diag = kt >= 4 * qb
                P.op("pe", lambda e, ps=ps, h=h, ks_=ks_, qs_=qs_: e.matmul(
                    ps[:], lhsT=kn[:, h, ks_], rhs=qn[:, h, qs_], start=True, stop=False),
                    reads=[P.R("a_kn"), P.R("a_qn")], writes=[pr])
                P.op("pe", lambda e, ps=ps, h=h, ks_=ks_, qs_=qs_, diag=diag: e.matmul(
                    ps[:], lhsT=kr[:, h, ks_], rhs=qr[:, h, qs_], start=False, stop=(not diag)),
                    reads=[P.R("a_kr"), P.R("a_qr")], writes=[pr])
                if diag:
                    j = kt - 4 * qb
                    P.op("pe", lambda e, ps=ps, j=j: e.matmul(ps[:], lhsT=C.ident[:], rhs=C.amask[:, j, :], start=False, stop=True),
                         reads=[P.R("c_ident"), P.R("c_amask")], writes=[pr])
                p_, pres = pT[pi % 3], P.R("a_pT", pi % 3)
                pi += 1
                P.op("act", lambda e, ps=ps, p_=p_: e.activation(out=p_[:], in_=ps[:], func=AF.Exp, scale=scale),
                     reads=[pr], writes=[pres])
                P.op("pe", lambda e, po=po, h=h, kt=kt, p_=p_, nkt=nkt: e.matmul(
                    po[:], lhsT=v[:, kt, h * 128:(h + 1) * 128], rhs=p_[:], start=(kt == 0), stop=(kt == nkt - 1)),
                    reads=[P.R("a_v"), pres], writes=[por])
                P.op("pe", lambda e, pd=pd, kt=kt, p_=p_, nkt=nkt: e.matmul(
                    pd[:], lhsT=C.ones_bf[:], rhs=p_[:], start=(kt == 0), stop=(kt == nkt - 1)),
                    reads=[P.R("ones_bf"), pres], writes=[pdr])
            P.op("dve", lambda e, pd=pd: e.reciprocal(out=rden[:], in_=pd[:]), reads=[pdr], writes=[P.R("a_rden")])
            o_, ores = ost[oi % 2], P.R("a_ost", oi % 2)
            oi += 1
            P.op("dve", lambda e, po=po, o_=o_: e.tensor_tensor(out=o_[:], in0=po[:], in1=rden[:], op=ALU.mult),
                 reads=[por, P.R("a_rden")], writes=[ores])
            P.dma("sp", io["oT"][h * 128:(h + 1) * 128, qs_], o_[:], reads=[ores], writes=[P.R("a_out")])


TB = 512


def merge_stage(P, C, PS, io, T, D=2048):
    KD = D // 128
    ys = [P.sb("m_y%d" % i, [128, 8, TB], BF16) for i in range(3)]
    wbr = [[P.sb("m_wbr%d_%d" % (i, j), [128, 8, 512], BF16) for j in range(2)] for i in range(3)]
    gch = [P.sb("m_g%d" % j, [128, 3, 4, TB], BF16) for j in range(2)]
    wo = [P.sb("m_wo%d" % j, [128, KD, 512], BF16) for j in range(2)]
    merged = P.sb("m_merged", [128, KD, TB], BF16)
    x1 = [P.sb("m_x1_%d" % j, [128, 4, TB], F32) for j in range(2)]
    tt = [P.sb("m_t%d" % j, [128, TB], F32) for j in range(4)]
    Wb = [io[k].rearrange("(k p) c -> p k c", p=128) for k in ("w_br_ssm", "w_br_mla", "w_br_ret")]
    Wo = io["w_out"].rearrange("(k p) c -> p k c", p=128)
    yin = [io[k].rearrange("(k p) t -> p k t", p=128) for k in ("ysT", "ymT", "yrT")]
    gv = io["gT"].rearrange("(i k p) t -> p i k t", i=3, p=128)
    xv = io["x1T"].rearrange("(k p) t -> p k t", p=128)
    ov = io["x2T"].rearrange("(k p) t -> p k t", p=128)
    ci = 0
    ti = 0
    for b in range(T // TB):
        ts = slice(b * TB, (b + 1) * TB)
        for i in range(3):
            P.dma("sp", ys[i][:], yin[i][:, :, ts], writes=[P.R("m_y", i)])
        for dc in range(4):
            wb2 = ci % 2
            ci += 1
            for i in range(3):
                P.dma("pool", wbr[i][wb2][:], Wb[i][:, :, dc * 512:(dc + 1) * 512], writes=[P.R("m_wbr", i, wb2)])
            for i in range(3):
                P.dma("sp", gch[wb2][:, i, :, :], gv[:, i, dc * 4:(dc + 1) * 4, ts], writes=[P.R("m_g", wb2, i)])
            for j in range(4):
                tl = []
                for i in range(3):
                    ps, pr = PS.get()
                    for k in range(8):
                        P.op("pe", lambda e, ps=ps, i=i, k=k, j=j, wb2=wb2: e.matmul(
                            ps[:], lhsT=wbr[i][wb2][:, k, j * 128:(j + 1) * 128], rhs=ys[i][:, k, :],
                            start=(k == 0), stop=(k == 7)), reads=[P.R("m_wbr", i, wb2), P.R("m_y", i)], writes=[pr])
                    t_, tr = tt[ti % 4], P.R("m_t", ti % 4)
                    ti += 1
                    P.op("dve", lambda e, ps=ps, t_=t_, i=i, j=j, wb2=wb2: e.tensor_tensor(
                        out=t_[:], in0=ps[:], in1=gch[wb2][:, i, j, :], op=ALU.mult),
                        reads=[pr, P.R("m_g", wb2, i)], writes=[tr])
                    tl.append((t_, tr))
                P.op("pool", lambda e, tl=tl: e.tensor_tensor(out=tl[0][0][:], in0=tl[0][0][:], in1=tl[1][0][:], op=ALU.add),
                     reads=[tl[0][1], tl[1][1]], writes=[tl[0][1]])
                P.op("pool", lambda e, tl=tl, dc=dc, j=j: e.tensor_tensor(
                    out=merged[:, dc * 4 + j, :], in0=tl[0][0][:], in1=tl[2][0][:], op=ALU.add),
                    reads=[tl[0][1], tl[2][1]], writes=[P.R("m_merged", dc * 4 + j)])
        for dc in range(4):
            wb2 = ci % 2
            ci += 1
            P.dma("pool", wo[wb2][:], Wo[:, :, dc * 512:(dc + 1) * 512], writes=[P.R("m_wo", wb2)])
            P.dma("sp", x1[wb2][:], xv[:, dc * 4:(dc + 1) * 4, ts], writes=[P.R("m_x1", wb2)])
            for j in range(4):
                ps, pr = PS.get()
                for k in range(KD):
                    P.op("pe", lambda e, ps=ps, k=k, j=j, wb2=wb2: e.matmul(
                        ps[:], lhsT=wo[wb2][:, k, j * 128:(j + 1) * 128], rhs=merged[:, k, :],
                        start=(k == 0), stop=(k == KD - 1)), reads=[P.R("m_wo", wb2), P.R("m_merged", k)], writes=[pr])
                P.op("dve", lambda e, ps=ps, j=j, wb2=wb2: e.tensor_tensor(
                    out=x1[wb2][:, j, :], in0=x1[wb2][:, j, :], in1=ps[:], op=ALU.add),
                    reads=[pr, P.R("m_x1", wb2)], writes=[P.R("m_x1", wb2)])
            P.dma("sp", ov[:, dc * 4:(dc + 1) * 4, ts], x1[wb2][:], reads=[P.R("m_x1", wb2)], writes=[P.R("m_out")])


D = 2048
FF = 5632
DIN = 14160
NC_USED = 4
DEPTH = 2


def pk(v, k):
    return np.ascontiguousarray(np.asarray(v).reshape(k, 128).T)


def build_fused(S, depth=DEPTH):
    nc = bass.Bass("TRN2", target_bir_lowering=False)

    def din(name, shape, dt=F32):
        return nc.dram_tensor(name, list(shape), dt, kind="ExternalInput").ap()

    def dint(name, shape, dt=BF16):
        return nc.dram_tensor(name, list(shape), dt, kind="Internal").ap()
    L = depth
    I = dict(
        xT=din("xT", [D, S]), pos=din("pos", [1, S], I32),
        ffn1_nw=din("ffn1_nw", [L, 128, 16]), ffn2_nw=din("ffn2_nw", [L, 128, 16]),
        ffn1_w_gate=din("ffn1_w_gate", [L, D, FF]), ffn1_w_up=din("ffn1_w_up", [L, D, FF]), ffn1_w_down=din("ffn1_w_down", [L, FF, D]),
        ffn2_w_gate=din("ffn2_w_gate", [L, D, FF]), ffn2_w_up=din("ffn2_w_up", [L, D, FF]), ffn2_w_down=din("ffn2_w_down", [L, FF, D]),
        mixw=din("mixw", [L, 128, 16]), w_in=din("w_in", [L, D, DIN]), qan=din("qan", [L, 128, 6]), kvan=din("kvan", [L, 128, 4]),
        nrm=din("nrm", [L, 128, 4]), gate_b=din("gate_b", [L, 128, 48]),
        wqb_n=din("wqb_n", [L, 768, 1024]), wqb_r=din("wqb_r", [L, 768, 512]),
        wkvb_k=din("wkvb_k", [L, 512, 1024]), wkvb_v=din("wkvb_v", [L, 512, 1024]),
        convw=din("convw", [L, 2, 128, 6, 4]), convb=din("convb", [L, 2, 128, 6]), dskip=din("dskip", [L, 2, 128, 4]),
        ssmw=din("ssmw", [L, 2, 128, 4]), dtb_c=din("dtb_c", [L, 2, 8, 1]), alog_c=din("alog_c", [L, 2, 8, 1]),
        dtb_r=din("dtb_r", [L, 2, 128, 8]), alog_r=din("alog_r", [L, 2, 128, 8]),
        ret_dm=din("ret_dm", [2, 128, 2, 128]), ret_qd=din("ret_qd", [2, 128, 2, 128]), ret_kd=din("ret_kd", [2, 128, 2]),
        ret_cd=din("ret_cd", [2, 128, 2]), retw=din("retw", [L, 2, 128, 4]),
        w_br_ssm=din("w_br_ssm", [L, 1024, D]), w_br_mla=din("w_br_mla", [L, 1024, D]), w_br_ret=din("w_br_ret", [L, 1024, D]),
        w_out=din("w_out", [L, D, D]))
    cdram = declare_consts(nc)
    yT = nc.dram_tensor("yT", [D, S], F32, kind="ExternalOutput").ap()
    xa = dint("xa", [D, S], F32)
    xb = dint("xb", [D, S], F32)
    xc_ = dint("xc_", [D, S], F32)
    T_ = {}
    for k, s in dict(zsT=[1024, S], xbcT=[1536, S], qnT=[1024, S], qrT=[512, S], knT=[1024, S], krT=[512, S],
                     vtm=[S, 1024], rqT=[1024, S], rkT=[1024, S], rvtm=[S, 1024], rgsT=[1024, S], gT=[6144, S],
                     ysT=[1024, S], ymT=[1024, S], yrT=[1024, S]).items():
        T_[k] = dint("i_" + k, s, BF16)
    T_["dtT"] = dint("i_dtT", [16, S], F32)
    T_["dttm"] = dint("i_dttm", [S, 16], F32)
    P = Prog(nc)
    for l in range(L):
        xin = I["xT"] if l == 0 else xc_
        C = Consts(P, cdram)
        ffn_stage(P, C, None, "f", xin, xa, I["ffn1_nw"][l], I["ffn1_w_gate"][l], I["ffn1_w_up"][l], I["ffn1_w_down"][l], S, D=D, FF=FF)
        P.end_stage()
        C = Consts(P, cdram, need=("invf", "rmat"))
        io = dict(T_, xT=xa, mixw=I["mixw"][l], w_in=I["w_in"][l], pos=I["pos"], qan=I["qan"][l], kvan=I["kvan"][l],
                  nrm=I["nrm"][l], gate_b=I["gate_b"][l], wqb_n=I["wqb_n"][l], wqb_r=I["wqb_r"][l],
                  wkvb_k=I["wkvb_k"][l], wkvb_v=I["wkvb_v"][l])
        proj_stage(P, C, PsPool(P, 8), io, S)
        P.end_stage()
        for hh in range(2):
            C = Consts(P, cdram, need=("ident", "negtri", "sel", "tri", "ones32"))
            io = dict(xT=T_["xbcT"][hh * 512:(hh + 1) * 512, :], BT=T_["xbcT"][1024 + hh * 128:1024 + (hh + 1) * 128, :],
                      CT=T_["xbcT"][1280 + hh * 128:1280 + (hh + 1) * 128, :], zsT=T_["zsT"][hh * 512:(hh + 1) * 512, :],
                      dtT=T_["dtT"][hh * 8:(hh + 1) * 8, :], dttm=T_["dttm"][:, hh * 8:(hh + 1) * 8],
                      convw=I["convw"][l, hh], convb=I["convb"][l, hh], dskip=I["dskip"][l, hh], ssmw=I["ssmw"][l, hh],
                      dtb_c=I["dtb_c"][l, hh], alog_c=I["alog_c"][l, hh], dtb_r=I["dtb_r"][l, hh], alog_r=I["alog_r"][l, hh],
                      yT=T_["ysT"][hh * 512:(hh + 1) * 512, :])
            ssd_stage(P, C, PsPool(P, 8), None, io, S)
            P.end_stage()
            C = Consts(P, cdram, need=("ident",))
            io = dict(rqT=T_["rqT"][hh * 512:(hh + 1) * 512, :], rkT=T_["rkT"][hh * 512:(hh + 1) * 512, :],
                      rgsT=T_["rgsT"][hh * 512:(hh + 1) * 512, :], rvtm=T_["rvtm"][:, hh * 512:(hh + 1) * 512],
                      ret_dm=I["ret_dm"][hh], ret_qd=I["ret_qd"][hh], ret_kd=I["ret_kd"][hh], ret_cd=I["ret_cd"][hh],
                      retw=I["retw"][l, hh], yT=T_["yrT"][hh * 512:(hh + 1) * 512, :])
            ret_stage(P, C, PsPool(P, 8), None, io, S)
            P.end_stage()
            C = Consts(P, cdram, need=("ident", "amask"))
            io = dict(qnT=T_["qnT"][hh * 512:(hh + 1) * 512, :], knT=T_["knT"][hh * 512:(hh + 1) * 512, :],
                      qrT=T_["qrT"][hh * 256:(hh + 1) * 256, :], krT=T_["krT"][hh * 256:(hh + 1) * 256, :],
                      vtm=T_["vtm"][:, hh * 512:(hh + 1) * 512], oT=T_["ymT"][hh * 512:(hh + 1) * 512, :])
            attn_stage(P, C, PsPool(P, 4), io, S)
            P.end_stage()
        C = Consts(P, cdram)
        io = dict(x1T=xa, ysT=T_["ysT"], ymT=T_["ymT"], yrT=T_["yrT"], gT=T_["gT"], w_br_ssm=I["w_br_ssm"][l],
                  w_br_mla=I["w_br_mla"][l], w_br_ret=I["w_br_ret"][l], w_out=I["w_out"][l], x2T=xb)
        merge_stage(P, C, PsPool(P, 8), io, S)
        P.end_stage()
        C = Consts(P, cdram)
        xo = yT if l == L - 1 else xc_
        ffn_stage(P, C, None, "f", xb, xo, I["ffn2_nw"][l], I["ffn2_w_gate"][l], I["ffn2_w_up"][l], I["ffn2_w_down"][l], S, D=D, FF=FF)
        P.end_stage(final=(l == L - 1))
    return nc


def host_maps(inp, S, depth=DEPTH):
    hc = host_consts()
    L = depth
    com = {"c_" + n: hc[n] for n in CONST_SHAPES}
    f = lambda k: np.ascontiguousarray(inp[k][:L])
    for k in ("ffn1_w_gate", "ffn1_w_up", "ffn1_w_down", "ffn2_w_gate", "ffn2_w_up", "ffn2_w_down", "w_in",
              "w_br_ssm", "w_br_mla", "w_br_ret", "w_out"):
        com[k] = f(k)
    com["ffn1_nw"] = np.stack([pk(inp["ffn1_norm"][l], 16) for l in range(L)])
    com["ffn2_nw"] = np.stack([pk(inp["ffn2_norm"][l], 16) for l in range(L)])
    com["mixw"] = np.stack([pk(inp["mix_norm"][l], 16) for l in range(L)])
    com["qan"] = np.stack([pk(inp["q_a_norm"][l], 6) for l in range(L)])
    com["kvan"] = np.stack([pk(inp["kv_a_norm"][l], 4) for l in range(L)])
    com["gate_b"] = np.stack([pk(inp["gate_b"][l], 48) for l in range(L)])
    nrm = np.zeros((L, 128, 4), np.float32)
    for l in range(L):
        nrm[l, :, 0] = inp["q_norm"][l][:128]
        nrm[l, :, 1] = inp["k_norm"][l][:128]
        nrm[l, :64, 2] = inp["q_norm"][l][128:]
        nrm[l, :64, 3] = inp["k_norm"][l][128:]
    com["nrm"] = nrm
    wqb = inp["w_q_b"][:L].reshape(L, 768, 8, 192)
    wkvb = inp["w_kv_b"][:L].reshape(L, 512, 8, 256)
    com["wqb_n"] = np.ascontiguousarray(wqb[:, :, :, :128].reshape(L, 768, 1024))
    com["wqb_r"] = np.ascontiguousarray(wqb[:, :, :, 128:].reshape(L, 768, 512))
    com["wkvb_k"] = np.ascontiguousarray(wkvb[:, :, :, :128].reshape(L, 512, 1024))
    com["wkvb_v"] = np.ascontiguousarray(wkvb[:, :, :, 128:].reshape(L, 512, 1024))
    convw = np.zeros((L, 2, 128, 6, 4), np.float32)
    convb = np.zeros((L, 2, 128, 6), np.float32)
    dskip = np.zeros((L, 2, 128, 4), np.float32)
    ssmw = np.zeros((L, 2, 128, 4), np.float32)
    dtb_c = np.zeros((L, 2, 8, 1), np.float32)
    alog_c = np.zeros((L, 2, 8, 1), np.float32)
    dtb_r = np.zeros((L, 2, 128, 8), np.float32)
    alog_r = np.zeros((L, 2, 128, 8), np.float32)
    retw = np.zeros((L, 2, 128, 4), np.float32)
    for l in range(L):
        for hh in range(2):
            chans = np.concatenate([np.arange(hh * 512, hh * 512 + 512), 1024 + hh * 128 + np.arange(128),
                                    1280 + hh * 128 + np.arange(128)])
            convw[l, hh] = inp["conv_w"][l][:, chans].T.reshape(6, 128, 4).transpose(1, 0, 2)
            convb[l, hh] = inp["conv_b"][l][chans].reshape(6, 128).T
            hs = slice(hh * 8, hh * 8 + 8)
            dskip[l, hh] = np.repeat(inp["d_skip"][l][hs], 64).reshape(4, 128).T
            ssmw[l, hh] = pk(inp["ssm_norm"][l][hh * 512:hh * 512 + 512], 4)
            dtb_c[l, hh, :, 0] = inp["dt_bias"][l][hs]
            alog_c[l, hh, :, 0] = inp["a_log"][l][hs]
            dtb_r[l, hh] = np.broadcast_to(inp["dt_bias"][l][hs], (128, 8))
            alog_r[l, hh] = np.broadcast_to(inp["a_log"][l][hs], (128, 8))
            retw[l, hh] = pk(inp["ret_norm"][l][hh * 512:hh * 512 + 512], 4)
    com.update(convw=convw, convb=convb, dskip=dskip, ssmw=ssmw, dtb_c=dtb_c, alog_c=alog_c, dtb_r=dtb_r, alog_r=alog_r, retw=retw)
    com["ret_dm"] = np.stack([np.ascontiguousarray(hc["ret_dm"][:, hh * 2:hh * 2 + 2, :]) for hh in range(2)])
    com["ret_qd"] = np.stack([np.ascontiguousarray(hc["ret_qd"][:, hh * 2:hh * 2 + 2, :]) for hh in range(2)])
    com["ret_kd"] = np.stack([np.ascontiguousarray(hc["ret_kd"][:, hh * 2:hh * 2 + 2]) for hh in range(2)])
    cd = np.asarray(hc["ret_cd"], np.float32)
    com["ret_cd"] = np.stack([np.ascontiguousarray(np.broadcast_to(cd[hh * 2:hh * 2 + 2], (128, 2))) for hh in range(2)])
    maps = []
    for b in range(NC_USED):
        m = dict(com)
        m["xT"] = np.ascontiguousarray(inp["x"][b, :S, :].T)
        m["pos"] = np.ascontiguousarray(inp["positions"][b, :S].reshape(1, S))
        maps.append(m)
    return maps


def forward_fused(inp, S, depth=DEPTH):
    nc = build_fused(S, depth)
    maps = host_maps(inp, S, depth)
    res = run_bass_kernel_spmd(nc, maps, core_ids=list(range(NC_USED)))
    out = np.empty((NC_USED, S, D), np.float32)
    for b in range(NC_USED):
        out[b] = res.results[b]["yT"].T
    return out


def kernel(**inputs):
    inp = {k: np.asarray(v) for k, v in inputs.items()}
    return forward_fused(inp, 4096, depth=2)
```
